# Optimizing a Trainium2 kernel written in Bass

```python
import jax, jax.numpy as jnp
from jax import lax
import numpy as np

D_MODEL = 1024
BATCH = 16
SEQ = 2048
DEPTH = 1

N_Q_HEADS = 8
N_KV_HEADS = 2
HEAD_DIM = 64
ATTN_WIDTH = N_Q_HEADS * HEAD_DIM
KV_WIDTH = N_KV_HEADS * HEAD_DIM
WINDOW = 128
BLOCK = 128
ROPE_THETA = 500000.0
ROPE_DIM = HEAD_DIM // 4
CONV_WIDTH = D_MODEL - ATTN_WIDTH
CONV_GROUPS = 8
CONV_WIDTH_K = 31
MIX_WIDTH = ATTN_WIDTH + CONV_WIDTH
IN_WIDTH = ATTN_WIDTH + 2 * KV_WIDTH + 2 * CONV_WIDTH
PEER_HEADS = 8
N_KEYS = 128
N_EXPERTS = N_KEYS * N_KEYS
PEER_QUERY_DIM = 256
PEER_HALF = PEER_QUERY_DIM // 2
PEER_TOPK = 16
PEER_CHUNK = 128
PLE_DIM = 256
DEEPNORM_ALPHA = (2 * DEPTH) ** 0.25
DEEPNORM_BETA = (8 * DEPTH) ** -0.25
LN_EPS = 1e-5
NEG_INF = -1e30

kernel_name = "hymba_conv_swa_sink_peer_deepnorm_ple"


def layer_norm(x, g, b):
    xf = x.astype(jnp.float32)
    mu = jnp.mean(xf, axis=-1, keepdims=True)
    var = jnp.mean(jnp.square(xf - mu), axis=-1, keepdims=True)
    y = (xf - mu) * lax.rsqrt(var + LN_EPS) * g.astype(jnp.float32) + b.astype(jnp.float32)
    return y.astype(x.dtype)


def partial_rotary(t, positions):
    half = ROPE_DIM // 2
    inv_freq = ROPE_THETA ** (-jnp.arange(half, dtype=jnp.float32) * (2.0 / ROPE_DIM))
    ang = positions.astype(jnp.float32)[..., None] * inv_freq
    cos = jnp.cos(ang)[:, :, None, :]
    sin = jnp.sin(ang)[:, :, None, :]
    tr = t[..., :ROPE_DIM].astype(jnp.float32)
    t1, t2 = tr[..., :half], tr[..., half:]
    rot = jnp.concatenate([t1 * cos - t2 * sin, t2 * cos + t1 * sin], axis=-1).astype(t.dtype)
    return jnp.concatenate([rot, t[..., ROPE_DIM:]], axis=-1)


def sliding_window_attention(q, k, v, sinks):
    b, s = q.shape[0], q.shape[1]
    nb = s // BLOCK
    g = N_Q_HEADS // N_KV_HEADS
    qb = q.reshape(b, nb, BLOCK, N_KV_HEADS, g, HEAD_DIM)

    def band(t):
        tp = jnp.pad(t, ((0, 0), (BLOCK, 0), (0, 0), (0, 0)))
        tp = tp.reshape(b, nb + 1, BLOCK, N_KV_HEADS, HEAD_DIM)
        return jnp.concatenate([tp[:, :-1], tp[:, 1:]], axis=2)

    kb, vb = band(k), band(v)
    scores = jnp.einsum('bnqkgd,bnskd->bnkgqs', qb, kb).astype(jnp.float32) * (HEAD_DIM ** -0.5)
    qi = jnp.arange(BLOCK)[:, None]
    si = jnp.arange(2 * BLOCK)[None, :]
    diff = qi + BLOCK - si
    blk = jnp.arange(nb)[:, None, None]
    valid = (diff >= 0) & (diff < WINDOW) & (blk * BLOCK - BLOCK + si[None] >= 0)
    scores = jnp.where(valid[None, :, None, None], scores, NEG_INF)
    sink = sinks.astype(jnp.float32).reshape(1, 1, N_KV_HEADS, g, 1, 1)
    m = jnp.maximum(jnp.max(scores, axis=-1, keepdims=True), sink)
    e = jnp.exp(scores - m)
    denom = jnp.sum(e, axis=-1, keepdims=True) + jnp.exp(sink - m)
    probs = (e / denom).astype(v.dtype)
    out = jnp.einsum('bnkgqs,bnskd->bnqkgd', probs, vb)
    return out.reshape(b, s, ATTN_WIDTH)


def conv_group(a, gate, conv_w, conv_b, ln_g, ln_b):
    h = a * jax.nn.sigmoid(gate)
    h = lax.conv_general_dilated(
        h, conv_w[:, None, :].astype(h.dtype), window_strides=(1,),
        padding=((CONV_WIDTH_K - 1, 0),),
        dimension_numbers=('NWC', 'WIO', 'NWC'),
        feature_group_count=CONV_WIDTH) + conv_b
    h = layer_norm(h, ln_g, ln_b)
    return jax.nn.silu(h)


def peer(x, w_query, sub_keys, u_table, v_table):
    b, s, d = x.shape
    t = b * s
    xt = x.reshape(t, d)
    q = (xt @ w_query).reshape(t, PEER_HEADS, 2, PEER_HALF)
    sc = jnp.einsum('thpc,hpnc->thpn', q, sub_keys).astype(jnp.float32)
    top_s, top_i = lax.top_k(sc, PEER_TOPK)
    cand = top_s[:, :, 0, :, None] + top_s[:, :, 1, None, :]
    best_s, best_c = lax.top_k(cand.reshape(t, PEER_HEADS, PEER_TOPK * PEER_TOPK), PEER_TOPK)
    i1 = jnp.take_along_axis(top_i[:, :, 0], best_c // PEER_TOPK, axis=-1)
    i2 = jnp.take_along_axis(top_i[:, :, 1], best_c % PEER_TOPK, axis=-1)
    n_chunks = t // PEER_CHUNK
    ids = (i1 * N_KEYS + i2).reshape(n_chunks, PEER_CHUNK, PEER_HEADS * PEER_TOPK)
    gates = jax.nn.softmax(best_s, axis=-1).astype(x.dtype).reshape(n_chunks, PEER_CHUNK, PEER_HEADS * PEER_TOPK)
    xc = xt.reshape(n_chunks, PEER_CHUNK, d)

    def chunk(args):
        xk, idk, gk = args
        u = jnp.take(u_table, idk, axis=0)
        h = jnp.einsum('cd,ced->ce', xk, u)
        act = jax.nn.gelu(h, approximate=False) * gk
        return jnp.einsum('ce,ced->cd', act, jnp.take(v_table, idk, axis=0))

    out = lax.map(chunk, (xc, ids, gates))
    return out.reshape(b, s, d)


def setup_inputs(seed: int = 0) -> dict:
    key = jax.random.key(seed)
    ks = jax.random.split(key, 24)
    f32 = jnp.float32
    nrm = lambda k, shape, scale: jax.random.normal(k, shape, f32) * scale
    x = jax.random.normal(ks[0], (BATCH, SEQ, D_MODEL), f32)
    p = jax.random.normal(ks[1], (DEPTH, BATCH, SEQ, PLE_DIM), f32)
    offsets = jax.random.randint(ks[2], (BATCH,), 0, 1024, dtype=jnp.int32)
    positions = (offsets[:, None] + jnp.arange(SEQ, dtype=jnp.int32)[None, :]).astype(jnp.int32)
    return {
        "x": x,
        "p": p,
        "positions": positions,
        "w_in": nrm(ks[3], (DEPTH, D_MODEL, IN_WIDTH), D_MODEL ** -0.5),
        "sinks": nrm(ks[4], (DEPTH, N_Q_HEADS), 0.5),
        "conv_w": nrm(ks[5], (DEPTH, CONV_WIDTH_K, CONV_WIDTH), CONV_WIDTH_K ** -0.5),
        "conv_b": nrm(ks[6], (DEPTH, CONV_WIDTH), 0.01),
        "conv_ln_g": 1.0 + nrm(ks[7], (DEPTH, CONV_WIDTH), 0.02),
        "conv_ln_b": nrm(ks[8], (DEPTH, CONV_WIDTH), 0.01),
        "w_out": nrm(ks[9], (DEPTH, MIX_WIDTH, D_MODEL), DEEPNORM_BETA * MIX_WIDTH ** -0.5),
        "ln1_g": 1.0 + nrm(ks[10], (DEPTH, D_MODEL), 0.02),
        "ln1_b": nrm(ks[11], (DEPTH, D_MODEL), 0.01),
        "peer_query": nrm(ks[12], (DEPTH, D_MODEL, PEER_HEADS * PEER_QUERY_DIM), D_MODEL ** -0.5),
        "peer_keys": nrm(ks[13], (DEPTH, PEER_HEADS, 2, N_KEYS, PEER_HALF), PEER_HALF ** -0.5),
        "peer_u": nrm(ks[14], (DEPTH, N_EXPERTS, D_MODEL), D_MODEL ** -0.5),
        "peer_v": nrm(ks[15], (DEPTH, N_EXPERTS, D_MODEL), DEEPNORM_BETA * PEER_HEADS ** -0.5),
        "ple_proj": nrm(ks[16], (DEPTH, PLE_DIM, D_MODEL), DEEPNORM_BETA * PLE_DIM ** -0.5),
        "ple_gate": nrm(ks[17], (DEPTH, D_MODEL, D_MODEL), D_MODEL ** -0.5),
        "ln2_g": 1.0 + nrm(ks[18], (DEPTH, D_MODEL), 0.02),
        "ln2_b": nrm(ks[19], (DEPTH, D_MODEL), 0.01),
    }


def reference(x, p, positions, w_in, sinks, conv_w, conv_b, conv_ln_g, conv_ln_b, w_out,
              ln1_g, ln1_b, peer_query, peer_keys, peer_u, peer_v, ple_proj, ple_gate,
              ln2_g, ln2_b):
    b, s, _ = x.shape
    splits = [ATTN_WIDTH, ATTN_WIDTH + KV_WIDTH, ATTN_WIDTH + 2 * KV_WIDTH,
              ATTN_WIDTH + 2 * KV_WIDTH + CONV_WIDTH]
    for i in range(DEPTH):
        h = x @ w_in[i]
        q, k, v, ca, cg = jnp.split(h, splits, axis=-1)
        q = partial_rotary(q.reshape(b, s, N_Q_HEADS, HEAD_DIM), positions)
        k = partial_rotary(k.reshape(b, s, N_KV_HEADS, HEAD_DIM), positions)
        v = v.reshape(b, s, N_KV_HEADS, HEAD_DIM)
        attn = sliding_window_attention(q, k, v, sinks[i])
        conv = conv_group(ca, cg, conv_w[i], conv_b[i], conv_ln_g[i], conv_ln_b[i])
        mixed = jnp.concatenate([attn, conv], axis=-1) @ w_out[i]
        x = layer_norm(DEEPNORM_ALPHA * x + mixed, ln1_g[i], ln1_b[i])
        r = DEEPNORM_ALPHA * x + peer(x, peer_query[i], peer_keys[i], peer_u[i], peer_v[i])
        e = jax.nn.sigmoid(r @ ple_gate[i]) * (p[i] @ ple_proj[i])
        x = layer_norm(r + e, ln2_g[i], ln2_b[i])
    return x
```

```python
import math
from contextlib import ExitStack

import numpy as np
import concourse.bass as bass
import concourse.mybir as mybir
from concourse.bass_utils import run_bass_kernel_spmd

F32 = mybir.dt.float32
BF16 = mybir.dt.bfloat16
U32 = mybir.dt.uint32
I32 = mybir.dt.int32
AF = mybir.ActivationFunctionType
ALU = mybir.AluOpType
AX = mybir.AxisListType

NCORES = 8
TPC = 4096
MT = 256
NMT = TPC // MT
ALPHA = 2.0 ** 0.25
EPS = 1e-5
NEG = -1e30
PERM = [0, 4, 1, 5, 2, 6, 3, 7]
NSLOT = 4
TB = 16


class Buf:
    __slots__ = ("name", "w", "rs")

    def __init__(self, name=""):
        self.name = name
        self.w = None
        self.rs = []


class Sched:
    ENG = ("pe", "act", "dve", "pool", "sp")

    def __init__(self, nc, es, n_epochs):
        self.nc = nc
        self.eng = {"pe": nc.tensor, "act": nc.scalar, "dve": nc.vector,
                    "pool": nc.gpsimd, "sp": nc.sync}
        self.sems = {e: [es.enter_context(nc.semaphore(f"c_{e}_{i}")) for i in range(n_epochs)]
                     for e in self.ENG}
        self.epoch = 0
        self.cnt = {e: 0 for e in self.ENG}
        self.waited = {e: {} for e in self.ENG}
        self.n_inst = 0
        self.n_wait = 0

    def new_epoch(self):
        for e in self.ENG:
            self.cnt[e] = 0
        self.epoch += 1

    def _wait(self, eng, dep):
        if dep is None:
            return
        if dep[0] == "e":
            _, pe, ep, n = dep
            if pe == "pe" and eng == "pe":
                return
            key = ("e", pe, ep)
            sem = self.sems[pe][ep]
        else:
            _, sem, n = dep
            key = ("d", id(sem))
        if self.waited[eng].get(key, 0) >= n:
            return
        self.waited[eng][key] = n
        self.eng[eng].wait_ge(sem, n)
        self.n_wait += 1

    def _wait_all(self, eng, reads, writes, extra=()):
        best = {}
        deps = list(extra)
        for b in reads:
            deps.append(b.w)
        for b in writes:
            deps.append(b.w)
            deps.extend(b.rs)
        for d in deps:
            if d is None:
                continue
            key = (d[0], d[1], d[2]) if d[0] == "e" else (d[0], id(d[1]))
            if key not in best or best[key][-1] < d[-1]:
                best[key] = d
        for d in best.values():
            self._wait(eng, d)

    def op(self, eng, fn, reads=(), writes=()):
        self._wait_all(eng, reads, writes)
        inst = fn()
        self.cnt[eng] += 1
        n = self.cnt[eng]
        inst.then_inc(self.sems[eng][self.epoch], 1)
        me = ("e", eng, self.epoch, n)
        for b in reads:
            b.rs.append(me)
        for b in writes:
            b.w = me
            b.rs = []
        self.n_inst += 1
        return me

    def dma(self, q, st, out, in_, reads=(), writes=(), extra=()):
        self._wait_all(q, reads, writes, extra)
        inst = self.eng[q].dma_start(out=out, in_=in_)
        st[1] += 16
        inst.then_inc(st[0], 16)
        me = ("d", st[0], st[1])
        for b in reads:
            b.rs.append(me)
        for b in writes:
            b.w = me
            b.rs = []
        self.n_inst += 1
        return me

    def barrier(self, dma_states=()):
        for e in self.ENG:
            for p in self.ENG:
                if p != e and self.cnt[p] > 0:
                    self._wait(e, ("e", p, self.epoch, self.cnt[p]))
            for st in dma_states:
                if st[1] > 0:
                    self._wait(e, ("d", st[0], st[1]))


class T:
    def __init__(self, t, name):
        self.t = t
        self.b = Buf(name)


def build(nmt=NMT, dbg=None, stop=None):
    nc = bass.Bass("TRN2", target_bir_lowering=False)

    def din(name, shape, dt=F32):
        return nc.dram_tensor(name, shape, dt, kind="ExternalInput").ap()

    def dscr(name, shape, dt=BF16):
        return nc.dram_tensor(name, shape, dt, kind="Internal").ap()

    x_d = din("x", [TPC, 1024])
    xT_d = din("xT", [128, 8 * TPC])
    pT_d = din("pT", [128, 2 * TPC])
    pos_d = din("pos", [128, 32], I32)
    wqkv_d = din("wqkv", [128, 8 * 768])
    wc_d = din("wc", [128, 8 * 1024])
    wout_d = din("wout", [128, 8 * 1024])
    wq_d = din("wq", [128, 8 * 2048])
    keysT_d = din("keysT", [128, 16 * 128])
    pleg_d = din("pleg", [128, 8 * 1024])
    plep_d = din("plep", [128, 2 * 1024])
    U_d = din("U", [16384, 1024])
    V_d = din("V", [16384, 1024])
    sinks_d = din("sinks", [1, 8])
    convw_d = din("convw", [128, 4 * 31])
    convb_d = din("convb", [128, 4])
    clg_d = din("clg", [128, 4])
    clb_d = din("clb", [128, 4])
    ln1g_d = din("ln1g", [1, 1024])
    ln1b_d = din("ln1b", [1, 1024])
    ln2g_d = din("ln2g", [1, 1024])
    ln2b_d = din("ln2b", [1, 1024])
    ident_d = din("ident", [128, 128])
    iota_d = din("iota", [128, 128])
    maskg_d = din("maskg", [128, 256])
    maskf_d = din("maskf", [128, 256])
    invf_d = din("invf", [128, 8])
    out_d = nc.dram_tensor("out", [TPC, 1024], F32, kind="ExternalOutput").ap()

    xT_s = dscr("xT_s", [128, 8 * TPC])
    pT_s = dscr("pT_s", [128, 2 * TPC])
    wqkv_s = dscr("wqkv_s", [128, 8 * 768])
    wc_s = dscr("wc_s", [128, 8 * 1024])
    wout_s = dscr("wout_s", [128, 8 * 1024])
    wq_s = dscr("wq_s", [128, 8 * 2048])
    keysT_s = dscr("keysT_s", [128, 16 * 128])
    pleg_s = dscr("pleg_s", [128, 8 * 1024])
    plep_s = dscr("plep_s", [128, 2 * 1024])
    U_s = dscr("U_s", [16384, 1024])
    V_s = dscr("V_s", [16384, 1024])

    dbg_outs = {}

    es = ExitStack()
    with es:
        n_epochs = nmt + 1
        S = Sched(nc, es, n_epochs)

        def newsem(name):
            return [es.enter_context(nc.semaphore(name)), 0]

        uid = [0]

        def sbt(stack, name, shape, dt=F32):
            uid[0] += 1
            return T(stack.enter_context(nc.sbuf_tensor(f"s{uid[0]}_{name}", shape, dt)), name)

        def pst(stack, name, shape, dt=F32):
            uid[0] += 1
            return T(stack.enter_context(nc.psum_tensor(f"p{uid[0]}_{name}", shape, dt)), name)

        def dve(fn, reads, writes, **kw):
            return S.op("dve", lambda: getattr(nc.vector, fn)(**kw), reads, writes)

        def act(fn, reads, writes, **kw):
            return S.op("act", lambda: getattr(nc.scalar, fn)(**kw), reads, writes)

        def pool(fn, reads, writes, **kw):
            return S.op("pool", lambda: getattr(nc.gpsimd, fn)(**kw), reads, writes)

        def mm(out, lhsT, rhs, start, stop, reads, writes):
            return S.op("pe", lambda: nc.tensor.matmul(out, lhsT=lhsT, rhs=rhs, start=start, stop=stop),
                        reads, writes)

        def tr(out, in_, ident, reads, writes):
            return S.op("pe", lambda: nc.tensor.transpose(out, in_, ident), reads, writes)

        sem_ld = newsem("ld")
        sem_pA = newsem("pA")
        sem_pB = newsem("pB")
        sem_st = newsem("st")
        sem_dbg = newsem("dbg")
        sem_U = [newsem(f"U{i}") for i in range(NSLOT)]

        def dump(name, tile_ap, buf, shape, dt=F32):
            if dbg is None or name not in dbg:
                return
            d = nc.dram_tensor("dbg_" + name, list(shape), dt, kind="ExternalOutput").ap()
            dbg_outs[name] = d
            S.dma("sp", sem_dbg, d, tile_ap, reads=[buf])

        def cast_dma(dst, src, st):
            inst = nc.gpsimd.dma_start(out=dst, in_=src)
            st[1] += 16
            inst.then_inc(st[0], 16)

        def cast_rows(dst, src, st):
            rl = 2048 if dst.shape[1] % 2048 == 0 else 1536
            cast_dma(dst.rearrange("p (k t) -> (p k) t", t=rl), src.rearrange("p (k t) -> (p k) t", t=rl), st)

        sem_pA1 = newsem("pA1")
        for dst, src in ((wc_s, wc_d), (wqkv_s, wqkv_d), (wout_s, wout_d)):
            cast_rows(dst, src, sem_pA1)
        xTs3 = xT_s.rearrange("p (k t) -> p k t", k=8)
        xTd3 = xT_d.rearrange("p (k t) -> p k t", k=8)
        cast_dma(xTs3[:, :, 0:MT], xTd3[:, :, 0:MT], sem_pA1)
        depA1 = ("d", sem_pA1[0], sem_pA1[1])
        nc.gpsimd.wait_ge(sem_pA1[0], sem_pA1[1])
        for dst, src in ((wq_s, wq_d), (keysT_s, keysT_d)):
            cast_rows(dst, src, sem_pA1)
        depA2a = ("d", sem_pA1[0], sem_pA1[1])
        nc.gpsimd.wait_ge(sem_pA1[0], sem_pA1[1])
        NPB = 64
        RPB = 8192 // NPB
        for i in range(NPB):
            for tbl_s, tbl_d in ((U_s, U_d), (V_s, V_d)):
                vs = tbl_s.rearrange("(a b) d -> a (b d)", b=2)
                vd = tbl_d.rearrange("(a b) d -> a (b d)", b=2)
                if sem_pB[1] >= 32:
                    nc.gpsimd.wait_ge(sem_pB[0], sem_pB[1] - 16)
                cast_dma(vs[i * RPB:(i + 1) * RPB, :], vd[i * RPB:(i + 1) * RPB, :], sem_pB)
        depB = ("d", sem_pB[0], sem_pB[1])
        nc.gpsimd.wait_ge(sem_pB[0], sem_pB[1])
        for dst, src in ((pleg_s, pleg_d), (plep_s, plep_d), (pT_s, pT_d)):
            cast_rows(dst, src, sem_pA)
        cast_dma(xTs3[:, :, MT:TPC], xTd3[:, :, MT:TPC], sem_pA)
        depA = ("d", sem_pA[0], sem_pA[1])

        ident_f = sbt(es, "ident_f", [128, 128])
        ident_b = sbt(es, "ident_b", [128, 128], BF16)
        iota = sbt(es, "iota", [128, 128])
        maskg = sbt(es, "maskg", [128, 256])
        maskf = sbt(es, "maskf", [128, 256])
        invf = sbt(es, "invf", [128, 8])
        posi = sbt(es, "posi", [128, 32], I32)
        sinks = sbt(es, "sinks", [128, 8])
        convw = sbt(es, "convw", [128, 4, 31])
        convb = sbt(es, "convb", [128, 4])
        clg = sbt(es, "clg", [128, 4])
        clb = sbt(es, "clb", [128, 4])
        keysT = sbt(es, "keysT", [128, 16, 128], BF16)
        cos_all = sbt(es, "cos_all", [128, 32, 8])
        sin_all = sbt(es, "sin_all", [128, 32, 8])
        ones_m = sbt(es, "ones_m", [128, 128])
        kTa = sbt(es, "kTa", [128, 17, 128], BF16)
        kTb = sbt(es, "kTb", [128, 17, 128], BF16)
        Vseq = sbt(es, "Vseq", [128, 17, 128], BF16)
        hbuf = sbt(es, "hbuf", [128, 4, 286], BF16)
        diagw = sbt(es, "diagw", [128, 4, 31, 128], BF16)
        x1_sb = sbt(es, "x1_sb", [128, 2, 1024])
        x1T = sbt(es, "x1T", [128, 8, 256], BF16)
        I1T = sbt(es, "I1T", [128, 256])
        I2T = sbt(es, "I2T", [128, 256])
        iota_b = sbt(es, "iota_b", [128, 128], BF16)
        gT = sbt(es, "gT", [128, 256])
        Us = [sbt(es, f"Us{i}", [128, 8, 128], BF16) for i in range(NSLOT)]
        Vs = [sbt(es, f"Vs{i}", [128, 1024], BF16) for i in range(NSLOT)]

        ld_group = []
        sem_lp = [newsem(f"lp{i}") for i in range(5)]

        def ld(tile, src, extra=(), k=None, dst=None):
            if k is not None:
                S.dma("sp", sem_lp[k], tile.t[:] if dst is None else dst, src, writes=[tile.b], extra=extra)
                return
            S.dma("sp", sem_ld, tile.t[:], src, writes=[tile.b], extra=extra)
            ld_group.append(tile)

        def ld_commit():
            for tl in ld_group:
                tl.b.w = ("d", sem_ld[0], sem_ld[1])
            ld_group.clear()

        ld(ident_f, ident_d[:, :])
        ld(iota, iota_d[:, :])
        ld(maskg, maskg_d[:, :])
        ld(maskf, maskf_d[:, :])
        ld(invf, invf_d[:, :])
        ld(posi, pos_d[:, :])
        ld(sinks, sinks_d[0:1, :].to_broadcast([128, 8]))
        ld(convw, convw_d.rearrange("p (c k) -> p c k", c=4))
        ld(convb, convb_d[:, :])
        ld(clg, clg_d[:, :])
        ld(clb, clb_d[:, :])
        ld_commit()

        with ExitStack() as ps_:
            dve("tensor_copy", [ident_f.b], [ident_b.b], out=ident_b.t[:], in_=ident_f.t[:])
            dve("tensor_copy", [iota.b], [iota_b.b], out=iota_b.t[:], in_=iota.t[:])
            dve("memset", [], [ones_m.b], ap=ones_m.t[:], constant=1.0 / 512)
            dve("memset", [], [kTa.b], ap=kTa.t[:], constant=0.0)
            dve("memset", [], [kTb.b], ap=kTb.t[:], constant=0.0)
            dve("memset", [], [Vseq.b], ap=Vseq.t[:], constant=0.0)
            dve("memset", [], [hbuf.b], ap=hbuf.t[:], constant=0.0)
            for cc in range(4):
                for k in range(31):
                    dve("tensor_scalar", [ident_f.b, convw.b], [diagw.b], out=diagw.t[:, cc, k, :], in0=ident_f.t[:],
                        scalar1=convw.t[:, cc, k:k + 1], scalar2=None, op0=ALU.mult)
            posf = sbt(ps_, "posf", [128, 32])
            ang = sbt(ps_, "ang", [128, 32, 8])
            tq = sbt(ps_, "tq", [128, 32, 8])
            ki = sbt(ps_, "ki", [128, 32, 8], I32)
            dve("tensor_copy", [posi.b], [posf.b], out=posf.t[:], in_=posi.t[:])
            dve("tensor_tensor", [posf.b, invf.b], [ang.b], out=ang.t[:],
                in0=posf.t[:].unsqueeze(2).to_broadcast([128, 32, 8]),
                in1=invf.t[:].unsqueeze(1).to_broadcast([128, 32, 8]), op=ALU.mult)
            TWO_PI = 2.0 * math.pi

            def range_reduce_sin(dst, shift):
                u = sbt(ps_, f"u{shift:.2f}", [128, 32, 8])
                dve("tensor_scalar", [ang.b], [u.b], out=u.t[:], in0=ang.t[:], scalar1=float(shift), scalar2=None,
                    op0=ALU.add)
                dve("tensor_scalar", [u.b], [tq.b], out=tq.t[:], in0=u.t[:], scalar1=1.0 / TWO_PI, scalar2=None,
                    op0=ALU.mult)
                dve("tensor_copy", [tq.b], [ki.b], out=ki.t[:], in_=tq.t[:])
                dve("tensor_copy", [ki.b], [tq.b], out=tq.t[:], in_=ki.t[:])
                dve("scalar_tensor_tensor", [tq.b, u.b], [u.b], out=u.t[:], in0=tq.t[:], scalar=-TWO_PI,
                    in1=u.t[:], op0=ALU.mult, op1=ALU.add)
                dve("tensor_single_scalar", [u.b], [tq.b], out=tq.t[:], in_=u.t[:], scalar=math.pi, op=ALU.is_gt)
                dve("scalar_tensor_tensor", [tq.b, u.b], [u.b], out=u.t[:], in0=tq.t[:], scalar=-TWO_PI,
                    in1=u.t[:], op0=ALU.mult, op1=ALU.add)
                dve("tensor_single_scalar", [u.b], [tq.b], out=tq.t[:], in_=u.t[:], scalar=-math.pi, op=ALU.is_lt)
                dve("scalar_tensor_tensor", [tq.b, u.b], [u.b], out=u.t[:], in0=tq.t[:], scalar=TWO_PI,
                    in1=u.t[:], op0=ALU.mult, op1=ALU.add)
                dve("tensor_scalar", [u.b], [u.b], out=u.t[:], in0=u.t[:], scalar1=-3.1415925, scalar2=3.1415925,
                    op0=ALU.max, op1=ALU.min)
                act("activation", [u.b], [dst.b], out=dst.t[:], in_=u.t[:], func=AF.Sin)

            range_reduce_sin(sin_all, 0.0)
            range_reduce_sin(cos_all, math.pi / 2)
            S.barrier([sem_ld])

        def ln_steps(stack, r_ap, r_b, g, b_, out_ap, out_buf, tag):
            stats = sbt(stack, "st_" + tag, [128, 2, 6])
            mv = sbt(stack, "mv_" + tag, [128, 2])
            rstd = sbt(stack, "rs_" + tag, [128, 1])
            for i in range(2):
                dve("bn_stats", [r_b], [stats.b], out=stats.t[:, i, :], in_=r_ap[:, i * 512:(i + 1) * 512])
                yield
            dve("bn_aggr", [stats.b], [mv.b], out=mv.t[:], in_=stats.t[:])
            yield
            dve("tensor_scalar", [mv.b], [rstd.b], out=rstd.t[:], in0=mv.t[:, 1:2], scalar1=EPS, scalar2=None,
                op0=ALU.add)
            yield
            act("activation", [rstd.b], [rstd.b], out=rstd.t[:], in_=rstd.t[:], func=AF.Ln)
            yield
            act("activation", [rstd.b], [rstd.b], out=rstd.t[:], in_=rstd.t[:], func=AF.Exp, scale=-0.5)
            yield
            dve("tensor_scalar", [r_b, mv.b, rstd.b], [r_b], out=r_ap, in0=r_ap, scalar1=mv.t[:, 0:1],
                scalar2=rstd.t[:, 0:1], op0=ALU.subtract, op1=ALU.mult)
            yield
            dve("tensor_tensor", [r_b, g.b], [r_b], out=r_ap, in0=r_ap, in1=g.t[:], op=ALU.mult)
            yield
            dve("tensor_tensor", [r_b, b_.b], [out_buf], out=out_ap, in0=r_ap, in1=b_.t[:], op=ALU.add)
            yield

        def run_interleaved(gens):
            gens = list(gens)
            while gens:
                for g_ in list(gens):
                    try:
                        next(g_)
                    except StopIteration:
                        gens.remove(g_)

        def layer_norm(stack, r, g, b_, out_ap, out_buf, tag):
            run_interleaved([ln_steps(stack, r.t[:], r.b, g, b_, out_ap, out_buf, tag)])

        def phase_M(mt):
            ms = mt % 8
            t0 = mt * MT
            with ExitStack() as pe_:
                wqkv = sbt(pe_, "wqkv", [128, 8, 768], BF16)
                wc = sbt(pe_, "wc", [128, 8, 1024], BF16)
                wout = sbt(pe_, "wout", [128, 8, 1024], BF16)
                xT = sbt(pe_, "xT", [128, 8, 256], BF16)
                x_sb = sbt(pe_, "x_sb", [128, 2, 1024])
                ld(xT, xT_s.rearrange("p (k t) -> p k t", k=8)[:, :, t0:t0 + MT],
                   extra=[depA1] if mt == 0 else [depA1, depA], k=0)
                ld(wc, wc_s.rearrange("p (k c) -> p k c", k=8), k=1)
                ld(wqkv, wqkv_s.rearrange("p (k c) -> p k c", k=8), k=2)
                ld(wout, wout_s.rearrange("p (k c) -> p k c", k=8), k=3)
                ld(x_sb, x_d[t0:t0 + MT, :].rearrange("(b p) d -> p b d", p=128), k=4)
                ln1g = sbt(pe_, "ln1g", [128, 1024])
                ln1b = sbt(pe_, "ln1b", [128, 1024])
                ld(ln1g, ln1g_d[0:1, :].to_broadcast([128, 1024]), k=4)
                ld(ln1b, ln1b_d[0:1, :].to_broadcast([128, 1024]), k=4)
                x_sb.b.w = ln1g.b.w = ln1b.b.w

                ps_qkv = pst(pe_, "ps_qkv", [128, 2, 512])
                ps_tr = pst(pe_, "ps_tr", [128, 8, 128], BF16)
                ps_sc = pst(pe_, "ps_sc", [128, 4, 256])
                ps_o = pst(pe_, "ps_o", [128, 8, 64])
                ps_c = [pst(pe_, "ps_c0", [128, 2, 256])] * 2
                ps_stat = pst(pe_, "ps_stat", [128, 2, 256])

                qkv_sb = sbt(pe_, "qkv_sb", [128, 768])
                rt = [sbt(pe_, f"rt{i}", [128, 10, 8]) for i in range(4)]
                qkv_bf = sbt(pe_, "qkv_bf", [128, 640], BF16)
                qT = sbt(pe_, "qT", [128, 4, 128], BF16)
                sc4 = sbt(pe_, "sc4", [128, 4, 256])
                P4 = [sbt(pe_, f"P4_{i}", [128, 4, 256], BF16) for i in range(2)]
                PT4 = [sbt(pe_, f"PT4_{i}", [128, 8, 128], BF16) for i in range(2)]
                m4 = [sbt(pe_, f"m4_{i}", [128, 7, 4]) for i in range(2)]
                cat_bf = sbt(pe_, "cat_bf", [128, 512], BF16)
                catT = sbt(pe_, "catT", [128, 8, 256], BF16)
                sig = [sbt(pe_, "sig0", [128, 256])] * 2
                cacc = [sbt(pe_, f"cacc{i}", [128, 256]) for i in range(4)]
                ysq = sbt(pe_, "ysq", [128, 256])
                mean_sb = sbt(pe_, "mean_sb", [128, 256])
                rstd_c = sbt(pe_, "rstd_c", [128, 256])
                r_sb = sbt(pe_, "r_sb", [128, 1024])
                x1_bf = sbt(pe_, "x1_bf", [128, 1024], BF16)
                hb = [Buf(f"hb{i}") for i in range(4)]

                if ms == 0:
                    for cc in range(4):
                        dve("memset", [hbuf.b], [hb[cc]], ap=hbuf.t[:, cc, 0:30], constant=0.0)
                else:
                    for cc in range(4):
                        dve("tensor_copy", [hbuf.b], [hb[cc]], out=hbuf.t[:, cc, 0:30], in_=hbuf.t[:, cc, 256:286])

                for cc in range(4):
                    pc = ps_c[cc % 2]
                    for j, col0 in enumerate((cc * 128, 512 + cc * 128)):
                        for kc in range(8):
                            mm(pc.t[:, j, :], wc.t[:, kc, col0:col0 + 128], xT.t[:, kc, :], kc == 0, kc == 7,
                               [wc.b, xT.b], [pc.b])
                    sg = sig[cc % 2]
                    act("activation", [pc.b], [sg.b], out=sg.t[:], in_=pc.t[:, 1, :], func=AF.Sigmoid)
                    dve("tensor_tensor", [pc.b, sg.b, hb[cc]], [hb[cc]], out=hbuf.t[:, cc, 30:286], in0=pc.t[:, 0, :],
                        in1=sg.t[:], op=ALU.mult)
                ps_cv = ps_qkv.t[:].rearrange("p a (b t) -> p (a b) t", b=2)
                for cc in range(4):
                    for k in range(31):
                        mm(ps_cv[:, cc, :], diagw.t[:, cc, k, :], hbuf.t[:, cc, k:k + 256], k == 0, k == 30,
                           [diagw.b, hb[cc]], [ps_qkv.b])
                for cc in range(4):
                    act("activation", [ps_qkv.b, convb.b], [cacc[cc].b], out=cacc[cc].t[:], in_=ps_cv[:, cc, :],
                        func=AF.Identity, bias=convb.t[:, cc:cc + 1], scale=1.0)

                for cc in range(4):
                    mm(ps_stat.t[:, 0, :], ones_m.t[:], cacc[cc].t[:], cc == 0, cc == 3, [ones_m.b, cacc[cc].b],
                       [ps_stat.b])
                for cc in range(4):
                    act("activation", [cacc[cc].b], [ysq.b], out=ysq.t[:], in_=cacc[cc].t[:], func=AF.Square)
                    mm(ps_stat.t[:, 1, :], ones_m.t[:], ysq.t[:], cc == 0, cc == 3, [ones_m.b, ysq.b], [ps_stat.b])
                act("copy", [ps_stat.b], [mean_sb.b], out=mean_sb.t[:], in_=ps_stat.t[:, 0, :])
                dve("tensor_tensor", [mean_sb.b], [ysq.b], out=ysq.t[:], in0=mean_sb.t[:], in1=mean_sb.t[:], op=ALU.mult)
                dve("scalar_tensor_tensor", [ps_stat.b, ysq.b], [rstd_c.b], out=rstd_c.t[:], in0=ps_stat.t[:, 1, :],
                    scalar=EPS, in1=ysq.t[:], op0=ALU.add, op1=ALU.subtract)
                act("activation", [rstd_c.b], [rstd_c.b], out=rstd_c.t[:], in_=rstd_c.t[:], func=AF.Ln)
                act("activation", [rstd_c.b], [rstd_c.b], out=rstd_c.t[:], in_=rstd_c.t[:], func=AF.Exp, scale=-0.5)
                for cc in range(4):
                    dve("tensor_tensor", [cacc[cc].b, mean_sb.b], [cacc[cc].b], out=cacc[cc].t[:], in0=cacc[cc].t[:],
                        in1=mean_sb.t[:], op=ALU.subtract)
                    dve("tensor_tensor", [cacc[cc].b, rstd_c.b], [cacc[cc].b], out=cacc[cc].t[:], in0=cacc[cc].t[:],
                        in1=rstd_c.t[:], op=ALU.mult)
                    act("activation", [cacc[cc].b, clg.b, clb.b], [catT.b], out=catT.t[:, 4 + cc, :], in_=cacc[cc].t[:],
                        func=AF.Silu, scale=clg.t[:, cc:cc + 1], bias=clb.t[:, cc:cc + 1])
                for b in range(2):
                    gblk = 2 * ms + b
                    sl = gblk + 1
                    blk = mt * 2 + b
                    for kc in range(8):
                        mm(ps_qkv.t[:, 0, :], xT.t[:, kc, b * 128:(b + 1) * 128], wqkv.t[:, kc, 0:512], kc == 0,
                           kc == 7, [xT.b, wqkv.b], [ps_qkv.b])
                    for kc in range(8):
                        mm(ps_qkv.t[:, 1, 0:256], xT.t[:, kc, b * 128:(b + 1) * 128], wqkv.t[:, kc, 512:768],
                           kc == 0, kc == 7, [xT.b, wqkv.b], [ps_qkv.b])
                    act("copy", [ps_qkv.b], [qkv_sb.b], out=qkv_sb.t[:, 0:512], in_=ps_qkv.t[:, 0, :])
                    act("copy", [ps_qkv.b], [qkv_sb.b], out=qkv_sb.t[:, 512:768], in_=ps_qkv.t[:, 1, 0:256])
                    qk = qkv_sb.t[:, 0:640].rearrange("p (h c) -> p h c", c=64)
                    t1 = qk[:, :, 0:8]
                    t2 = qk[:, :, 8:16]
                    cs = cos_all.t[:, blk, :].unsqueeze(1).to_broadcast([128, 10, 8])
                    sn = sin_all.t[:, blk, :].unsqueeze(1).to_broadcast([128, 10, 8])
                    dve("tensor_tensor", [qkv_sb.b, cos_all.b], [rt[0].b], out=rt[0].t[:], in0=t1, in1=cs, op=ALU.mult)
                    dve("tensor_tensor", [qkv_sb.b, sin_all.b], [rt[1].b], out=rt[1].t[:], in0=t2, in1=sn, op=ALU.mult)
                    dve("tensor_tensor", [qkv_sb.b, cos_all.b], [rt[2].b], out=rt[2].t[:], in0=t2, in1=cs, op=ALU.mult)
                    dve("tensor_tensor", [qkv_sb.b, sin_all.b], [rt[3].b], out=rt[3].t[:], in0=t1, in1=sn, op=ALU.mult)
                    dve("tensor_tensor", [rt[0].b, rt[1].b], [qkv_sb.b], out=t1, in0=rt[0].t[:], in1=rt[1].t[:],
                        op=ALU.subtract)
                    dve("tensor_tensor", [rt[2].b, rt[3].b], [qkv_sb.b], out=t2, in0=rt[2].t[:], in1=rt[3].t[:],
                        op=ALU.add)
                    act("copy", [qkv_sb.b], [qkv_bf.b], out=qkv_bf.t[:], in_=qkv_sb.t[:, 0:640])
                    act("copy", [qkv_sb.b], [Vseq.b], out=Vseq.t[:, sl, :], in_=qkv_sb.t[:, 640:768])
                    for j in range(5):
                        tr(ps_tr.t[:, j, :], qkv_bf.t[:, j * 128:(j + 1) * 128], ident_b.t[:],
                           [qkv_bf.b, ident_b.b], [ps_tr.b])
                    dve("tensor_copy", [ps_tr.b], [qT.b], out=qT.t[:], in_=ps_tr.t[:, 0:4, :])
                    dve("tensor_copy", [ps_tr.b], [kTa.b], out=kTa.t[0:64, sl, :], in_=ps_tr.t[0:64, 4, :])
                    dve("tensor_copy", [ps_tr.b], [kTb.b], out=kTb.t[64:128, sl, :], in_=ps_tr.t[64:128, 4, :])
                    if dbg is not None and mt == 0 and b == 0:
                        dump("qkv", qkv_sb.t[:], qkv_sb.b, [128, 768])
                    mask = maskf if gblk == 0 else maskg
                    for g in range(2):
                        P4g, PT4g, m4g = P4[g], PT4[g], m4[g]
                        for k4 in range(4):
                            hi = 4 * g + k4
                            jq, hf = hi // 2, hi % 2
                            kT = kTa if hf == 0 else kTb
                            mm(ps_sc.t[:, k4, :], qT.t[:, jq, :],
                               kT.t[:, sl - 1:sl + 1, :].rearrange("p a b -> p (a b)"), True, True, [qT.b, kT.b],
                               [ps_sc.b])
                        dve("scalar_tensor_tensor", [ps_sc.b, mask.b], [sc4.b], out=sc4.t[:], in0=ps_sc.t[:],
                            scalar=0.125, in1=mask.t[:].unsqueeze(1).to_broadcast([128, 4, 256]), op0=ALU.mult,
                            op1=ALU.add)
                        dve("tensor_reduce", [sc4.b], [m4g.b], out=m4g.t[:, 0, :], in_=sc4.t[:], axis=AX.X, op=ALU.max)
                        dve("tensor_tensor", [m4g.b, sinks.b], [m4g.b], out=m4g.t[:, 1, :], in0=m4g.t[:, 0, :],
                            in1=sinks.t[:, 4 * g:4 * g + 4], op=ALU.max)
                        dve("tensor_scalar", [m4g.b], [m4g.b], out=m4g.t[:, 2, :], in0=m4g.t[:, 1, :], scalar1=-1.0,
                            scalar2=None, op0=ALU.mult)
                        dve("tensor_tensor", [sc4.b, m4g.b], [sc4.b], out=sc4.t[:], in0=sc4.t[:],
                            in1=m4g.t[:, 2, :].unsqueeze(2).to_broadcast([128, 4, 256]), op=ALU.add)
                        act("activation", [sc4.b], [P4g.b], out=P4g.t[:], in_=sc4.t[:], func=AF.Exp)
                        dve("tensor_reduce", [P4g.b], [m4g.b], out=m4g.t[:, 3, :], in_=P4g.t[:], axis=AX.X, op=ALU.add)
                        dve("tensor_tensor", [m4g.b, sinks.b], [m4g.b], out=m4g.t[:, 4, :], in0=m4g.t[:, 2, :],
                            in1=sinks.t[:, 4 * g:4 * g + 4], op=ALU.add)
                        act("activation", [m4g.b], [m4g.b], out=m4g.t[:, 4, :], in_=m4g.t[:, 4, :], func=AF.Exp)
                        dve("tensor_tensor", [m4g.b], [m4g.b], out=m4g.t[:, 5, :], in0=m4g.t[:, 3, :], in1=m4g.t[:, 4, :],
                            op=ALU.add)
                        dve("reciprocal", [m4g.b], [m4g.b], out=m4g.t[:, 6, :], in_=m4g.t[:, 5, :])
                        for k4 in range(4):
                            for j2 in range(2):
                                tr(ps_tr.t[:, k4 * 2 + j2, :], P4g.t[:, k4, j2 * 128:(j2 + 1) * 128], ident_b.t[:],
                                   [P4g.b, ident_b.b], [ps_tr.b])
                        act("copy", [ps_tr.b], [PT4g.b], out=PT4g.t[:], in_=ps_tr.t[:])
                        for k4 in range(4):
                            hf = (4 * g + k4) % 2
                            for j2 in range(2):
                                mm(ps_o.t[:, 4 * g + k4, :], PT4g.t[:, k4 * 2 + j2, :],
                                   Vseq.t[:, sl - 1 + j2, hf * 64:(hf + 1) * 64], j2 == 0, j2 == 1, [PT4g.b, Vseq.b],
                                   [ps_o.b])
                        dve("tensor_tensor", [ps_o.b, m4g.b], [cat_bf.b],
                            out=cat_bf.t[:, g * 256:(g + 1) * 256].rearrange("p (k d) -> p k d", k=4),
                            in0=ps_o.t[:, 4 * g:4 * g + 4, :], in1=m4g.t[:, 6, :].unsqueeze(2).to_broadcast([128, 4, 64]),
                            op=ALU.mult)
                    for j in range(4):
                        tr(ps_tr.t[:, j, :], cat_bf.t[:, j * 128:(j + 1) * 128], ident_b.t[:],
                           [cat_bf.b, ident_b.b], [ps_tr.b])
                    act("copy", [ps_tr.b], [catT.b], out=catT.t[:, 0:4, b * 128:(b + 1) * 128], in_=ps_tr.t[:, 0:4, :])

                if dbg is not None and mt == 0:
                    dump("catT", catT.t[:].rearrange("p a b -> p (a b)"), catT.b, [128, 2048], BF16)

                ps_mix = [(ps_qkv.t[:].rearrange("p a b -> p (a b)"), ps_qkv.b),
                          (ps_sc.t[:].rearrange("p a b -> p (a b)"), ps_sc.b)]
                r_bufs = [(r_sb.t[:], r_sb.b), (sc4.t[:].rearrange("p a b -> p (a b)"), sc4.b)]
                for b in range(2):
                    pm_ap, pm_b = ps_mix[b]
                    for half in range(2):
                        for fc in range(8):
                            mm(pm_ap[:, half * 512:(half + 1) * 512], catT.t[:, fc, b * 128:(b + 1) * 128],
                               wout.t[:, fc, half * 512:(half + 1) * 512], fc == 0, fc == 7, [catT.b, wout.b], [pm_b])
                for b in range(2):
                    pm_ap, pm_b = ps_mix[b]
                    r_ap, r_b = r_bufs[b]
                    dve("scalar_tensor_tensor", [x_sb.b, pm_b], [r_b], out=r_ap, in0=x_sb.t[:, b, :],
                        scalar=ALPHA, in1=pm_ap, op0=ALU.mult, op1=ALU.add)
                x1b = [Buf("x1b0"), Buf("x1b1")]
                run_interleaved([ln_steps(pe_, r_bufs[b][0], r_bufs[b][1], ln1g, ln1b, x1_sb.t[:, b, :], x1b[b],
                                          f"m{b}") for b in range(2)])
                for b in range(2):
                    act("copy", [x1b[b]], [x1_bf.b], out=x1_bf.t[:], in_=x1_sb.t[:, b, :])
                    for j in range(8):
                        tr(ps_tr.t[:, j, :], x1_bf.t[:, j * 128:(j + 1) * 128], ident_b.t[:], [x1_bf.b, ident_b.b],
                           [ps_tr.b])
                    dve("tensor_copy", [ps_tr.b], [x1T.b], out=x1T.t[:, :, b * 128:(b + 1) * 128], in_=ps_tr.t[:])
                dve("memset", [x1b[0], x1b[1]], [x1_sb.b], ap=m4[0].t[:, 0, 0:1], constant=0.0)
                if dbg is not None and mt == 0:
                    dump("x1", x1_sb.t[:].rearrange("p a b -> p (a b)"), x1_sb.b, [128, 2048])
                hbuf.b.w = None
                hbuf.b.rs = []
                S.barrier([sem_ld, sem_dbg] + sem_lp)

        def phase_R(mt):
            with ExitStack() as pe_:
                wq = sbt(pe_, "wq", [128, 8, 2048], BF16)
                wq_parts = [T(wq.t, f"wq{i}") for i in range(4)]
                wq_v = wq_s.rearrange("p (k c) -> p k c", k=8)
                if mt == 0:
                    ld(keysT, keysT_s.rearrange("p (h n) -> p h n", h=16), extra=[depA2a], k=4)
                for i in range(4):
                    ld(wq_parts[i], wq_v[:, :, i * 512:(i + 1) * 512], extra=[depA2a], k=i,
                       dst=wq.t[:, :, i * 512:(i + 1) * 512])
                qpT = sbt(pe_, "qpT", [128, 16, 256], BF16)
                ps_qp = [pst(pe_, f"ps_qp{i}", [128, 256]) for i in range(2)]
                ps_s = [pst(pe_, f"ps_s{i}", [128, 4, 128]) for i in range(4)]
                ps_t3 = pst(pe_, "ps_t3", [128, 3, 128])
                sc = sbt(pe_, "sc", [128, 16, 128])
                scw = sbt(pe_, "scw", [128, 16, 128])
                top = sbt(pe_, "top", [128, 16, 16])
                idxu = sbt(pe_, "idxu", [128, 16, 16], U32)
                idxf = sbt(pe_, "idxf", [128, 16, 16])
                cand = sbt(pe_, "cand", [128, 8, 256])
                candw = sbt(pe_, "candw", [128, 8, 256])
                best = sbt(pe_, "best", [128, 8, 16])
                cidx = sbt(pe_, "cidx", [128, 8, 16], U32)
                kk = sbt(pe_, "kk", [128, 2, 128], U32)
                kkf = sbt(pe_, "kkf", [128, 2, 128])
                oh = sbt(pe_, "oh", [128, 8, 16, 16])
                IG = sbt(pe_, "IG", [128, 3, 128])
                ee = sbt(pe_, "ee", [128, 8, 16])
                zz = sbt(pe_, "zz", [128, 2, 8])

                for hp in range(16):
                    pq = ps_qp[hp % 2]
                    for kc in range(8):
                        mm(pq.t[:], wq.t[:, kc, hp * 128:(hp + 1) * 128], x1T.t[:, kc, :], kc == 0, kc == 7,
                           [wq_parts[hp // 4].b, x1T.b], [pq.b])
                    if hp % 2 == 0:
                        act("copy", [pq.b], [qpT.b], out=qpT.t[:, hp, :], in_=pq.t[:])
                    else:
                        dve("tensor_copy", [pq.b], [qpT.b], out=qpT.t[:, hp, :], in_=pq.t[:])
                for b in range(2):
                    for hp in range(16):
                        pss = ps_s[hp // 4]
                        mm(pss.t[:, hp % 4, :], qpT.t[:, hp, b * 128:(b + 1) * 128], keysT.t[:, hp, :], True, True,
                           [qpT.b, keysT.b], [pss.b])
                    for q4 in range(4):
                        act("copy", [ps_s[q4].b], [sc.b], out=sc.t[:, q4 * 4:(q4 + 1) * 4, :], in_=ps_s[q4].t[:])
                    if dbg is not None and mt == 0 and b == 0:
                        dump("sc", sc.t[:].rearrange("p a b -> p (a b)"), sc.b, [128, 2048])
                    tb = [Buf() for _ in range(16)]
                    ib = [Buf() for _ in range(16)]
                    wb = [Buf() for _ in range(16)]
                    for hp in range(16):
                        dve("max", [sc.b], [tb[hp]], out=top.t[:, hp, 0:8], in_=sc.t[:, hp, :])
                    for hp in range(16):
                        dve("max_index", [sc.b, tb[hp]], [ib[hp]], out=idxu.t[:, hp, 0:8], in_max=top.t[:, hp, 0:8],
                            in_values=sc.t[:, hp, :])
                    for hp in range(16):
                        dve("match_replace", [sc.b, tb[hp]], [wb[hp]], out=scw.t[:, hp, :],
                            in_to_replace=top.t[:, hp, 0:8], in_values=sc.t[:, hp, :], imm_value=NEG)
                    for hp in range(16):
                        dve("max", [wb[hp]], [tb[hp]], out=top.t[:, hp, 8:16], in_=scw.t[:, hp, :])
                    for hp in range(16):
                        dve("max_index", [wb[hp], tb[hp]], [ib[hp]], out=idxu.t[:, hp, 8:16], in_max=top.t[:, hp, 8:16],
                            in_values=scw.t[:, hp, :])
                    dve("tensor_copy", ib + [idxu.b], [idxf.b, idxu.b], out=idxf.t[:], in_=idxu.t[:])
                    top.b.w = None
                    top.b.rs = []
                    top4 = top.t[:].rearrange("p (h q) k -> p h q k", q=2)
                    idx4 = idxf.t[:].rearrange("p (h q) k -> p h q k", q=2)
                    cand4 = cand.t[:].rearrange("p h (a b) -> p h a b", a=16)
                    dve("tensor_tensor", tb + wb + [top.b, scw.b], [cand.b, top.b, scw.b], out=cand4,
                        in0=top4[:, :, 0, :].unsqueeze(3).to_broadcast([128, 8, 16, 16]),
                        in1=top4[:, :, 1, :].unsqueeze(2).to_broadcast([128, 8, 16, 16]), op=ALU.add)
                    bb_ = [Buf() for _ in range(8)]
                    cb_ = [Buf() for _ in range(8)]
                    cw_ = [Buf() for _ in range(8)]
                    for h in range(8):
                        dve("max", [cand.b], [bb_[h]], out=best.t[:, h, 0:8], in_=cand.t[:, h, :])
                    for h in range(8):
                        dve("max_index", [cand.b, bb_[h]], [cb_[h]], out=cidx.t[:, h, 0:8], in_max=best.t[:, h, 0:8],
                            in_values=cand.t[:, h, :])
                    for h in range(8):
                        dve("match_replace", [cand.b, bb_[h]], [cw_[h]], out=candw.t[:, h, :],
                            in_to_replace=best.t[:, h, 0:8], in_values=cand.t[:, h, :], imm_value=NEG)
                    for h in range(8):
                        dve("max", [cw_[h]], [bb_[h]], out=best.t[:, h, 8:16], in_=candw.t[:, h, :])
                    for h in range(8):
                        dve("max_index", [cw_[h], bb_[h]], [cb_[h]], out=cidx.t[:, h, 8:16],
                            in_max=best.t[:, h, 8:16], in_values=candw.t[:, h, :])
                    best.b.w = None
                    best.b.rs = []
                    cflat = cidx.t[:].rearrange("p h k -> p (h k)")
                    dve("tensor_single_scalar", cb_ + cw_ + [cidx.b, candw.b], [kk.b, cidx.b, candw.b], out=kk.t[:, 0, :],
                        in_=cflat, scalar=4, op=ALU.logical_shift_right)
                    dve("tensor_single_scalar", [cidx.b], [kk.b], out=kk.t[:, 1, :], in_=cflat, scalar=15,
                        op=ALU.bitwise_and)
                    dve("tensor_copy", [kk.b], [kkf.b], out=kkf.t[:], in_=kk.t[:])
                    for q in range(2):
                        kq = kkf.t[:, q, :].rearrange("p (h k) -> p h k", h=8)
                        dve("tensor_tensor", [kkf.b, iota.b], [oh.b], out=oh.t[:],
                            in0=kq.unsqueeze(3).to_broadcast([128, 8, 16, 16]),
                            in1=iota.t[:, 0:16].unsqueeze(1).unsqueeze(1).to_broadcast([128, 8, 16, 16]),
                            op=ALU.is_equal)
                        dve("tensor_tensor", [oh.b, idxf.b], [oh.b], out=oh.t[:], in0=oh.t[:],
                            in1=idx4[:, :, q, :].unsqueeze(2).to_broadcast([128, 8, 16, 16]), op=ALU.mult)
                        dve("tensor_reduce", [oh.b], [IG.b], out=IG.t[:, q, :].rearrange("p (h k) -> p h k", h=8),
                            in_=oh.t[:], axis=AX.X, op=ALU.add)
                    dve("tensor_tensor", bb_ + [best.b], [ee.b, best.b], out=ee.t[:], in0=best.t[:],
                        in1=best.t[:, :, 0:1].to_broadcast([128, 8, 16]), op=ALU.subtract)
                    act("activation", [ee.b], [ee.b], out=ee.t[:], in_=ee.t[:], func=AF.Exp)
                    dve("tensor_reduce", [ee.b], [zz.b], out=zz.t[:, 0, :], in_=ee.t[:], axis=AX.X, op=ALU.add)
                    dve("reciprocal", [zz.b], [zz.b], out=zz.t[:, 1, :], in_=zz.t[:, 0, :])
                    dve("tensor_tensor", [ee.b, zz.b], [IG.b], out=IG.t[:, 2, :].rearrange("p (h k) -> p h k", h=8),
                        in0=ee.t[:], in1=zz.t[:, 1, :].unsqueeze(2).to_broadcast([128, 8, 16]), op=ALU.mult)
                    if dbg is not None and mt == 0 and b == 0:
                        dump("IG", IG.t[:].rearrange("p a b -> p (a b)"), IG.b, [128, 384])
                    for q in range(3):
                        tr(ps_t3.t[:, q, :], IG.t[:, q, :], ident_f.t[:], [IG.b, ident_f.b], [ps_t3.b])
                    act("copy", [ps_t3.b], [I1T.b], out=I1T.t[:, b * 128:(b + 1) * 128], in_=ps_t3.t[:, 0, :])
                    act("copy", [ps_t3.b], [I2T.b], out=I2T.t[:, b * 128:(b + 1) * 128], in_=ps_t3.t[:, 1, :])
                    act("copy", [ps_t3.b], [gT.b], out=gT.t[:, b * 128:(b + 1) * 128], in_=ps_t3.t[:, 2, :])
                S.barrier([sem_ld, sem_dbg] + sem_lp)

        r2p = [sbt(es, f"r2p{i}", [128, 1024]) for i in range(2)]

        def fprime_steps(mtp, st_, ps_f):
            t0p = mtp * MT
            pleg = sbt(st_, "pleg", [128, 8, 1024], BF16)
            plep = sbt(st_, "plep", [128, 2, 1024], BF16)
            pT = sbt(st_, "pT", [128, 2, 256], BF16)
            ln2g = sbt(st_, "ln2g", [128, 1024])
            ln2b = sbt(st_, "ln2b", [128, 1024])
            ld(pleg, pleg_s.rearrange("p (k c) -> p k c", k=8), extra=[depA], k=0)
            ld(plep, plep_s.rearrange("p (k c) -> p k c", k=2), k=1)
            ld(pT, pT_s.rearrange("p (k t) -> p k t", k=2)[:, :, t0p:t0p + MT], k=2)
            ld(ln2g, ln2g_d[0:1, :].to_broadcast([128, 1024]), k=3)
            ld(ln2b, ln2b_d[0:1, :].to_broadcast([128, 1024]), k=3)
            ln2g.b.w = ln2b.b.w
            yield
            r2T = sbt(st_, "r2T", [128, 8, 128], BF16)
            sg = sbt(st_, "sgf", [128, 1024])
            o_sb = sbt(st_, "o_sb", [128, 1024])
            psf_flat = ps_f.t[:].rearrange("p a b -> p (a b)")
            psf_tr = ps_f.t[:].rearrange("p a (c d) -> p (a c) d", c=4)
            for b in range(2):
                for j in range(8):
                    tr(psf_tr[:, j, :], r2p[b].t[:, j * 128:(j + 1) * 128], ident_f.t[:], [r2p[b].b, ident_f.b],
                       [ps_f.b])
                    yield
                act("copy", [ps_f.b], [r2T.b], out=r2T.t[:], in_=psf_tr)
                yield
                for half in range(2):
                    for kc in range(8):
                        mm(ps_f.t[:, half, :], r2T.t[:, kc, :], pleg.t[:, kc, half * 512:(half + 1) * 512], kc == 0,
                           kc == 7, [r2T.b, pleg.b], [ps_f.b])
                        yield
                for hh in range(2):
                    act("activation", [ps_f.b], [sg.b], out=sg.t[:, hh * 512:(hh + 1) * 512], in_=ps_f.t[:, hh, :],
                        func=AF.Tanh, scale=0.5)
                    yield
                for half in range(2):
                    for kc in range(2):
                        mm(ps_f.t[:, half, :], pT.t[:, kc, b * 128:(b + 1) * 128],
                           plep.t[:, kc, half * 512:(half + 1) * 512], kc == 0, kc == 1, [pT.b, plep.b], [ps_f.b])
                        yield
                dve("scalar_tensor_tensor", [sg.b, ps_f.b], [sg.b], out=sg.t[:], in0=sg.t[:], scalar=1.0, in1=psf_flat,
                    op0=ALU.add, op1=ALU.mult)
                yield
                dve("scalar_tensor_tensor", [sg.b, r2p[b].b], [sg.b], out=sg.t[:], in0=sg.t[:], scalar=0.5,
                    in1=r2p[b].t[:], op0=ALU.mult, op1=ALU.add)
                yield
                yield from ln_steps(st_, sg.t[:], sg.b, ln2g, ln2b, o_sb.t[:], o_sb.b, f"f{b}")
                S.dma("sp", sem_st, out_d[t0p + b * 128:t0p + (b + 1) * 128, :], o_sb.t[:], reads=[o_sb.b])
                yield

        def phase_GEF(mt):
            with ExitStack() as po_:
                acc = pst(po_, "acc", [128, 4, 512])
                bacc = [Buf(f"acc{i}") for i in range(4)]
                G = sbt(po_, "G", [128, 256, 128], BF16)
                ps_H = [pst(po_, f"ps_H{i}", [128, 256]) for i in range(2)]
                gbufs = [Buf(f"G{i}") for i in range(MT // TB)]

                def emit_load(c):
                    s_ = c % NSLOT
                    S.dma("sp", sem_U[s_], Us[s_].t[:],
                          U_s[c * 128:(c + 1) * 128, :].rearrange("p (k e) -> p k e", k=8), writes=[Us[s_].b],
                          extra=[depB])
                    S.dma("sp", sem_U[s_], Vs[s_].t[:], V_s[c * 128:(c + 1) * 128, :], writes=[Vs[s_].b])
                    Us[s_].b.w = Vs[s_].b.w

                for c in range(NSLOT):
                    emit_load(c)
                with ExitStack() as pe_:
                    P1 = [sbt(pe_, f"P1_{i}", [128, TB, 128], BF16) for i in range(2)]
                    P2g = [sbt(pe_, f"P2g_{i}", [128, TB, 128], BF16) for i in range(2)]
                    ps_G = [pst(pe_, f"ps_G{i}", [128, 4, 128]) for i in range(2)]
                    gi = 0
                    for sbi in range(MT // TB):
                        tk = sbi * TB
                        i2 = sbi % 2
                        for tt in range(TB):
                            t_ = tk + tt
                            dve("tensor_scalar", [iota_b.b, I1T.b], [P1[i2].b], out=P1[i2].t[:, tt, :], in0=iota_b.t[:],
                                scalar1=I1T.t[:, t_:t_ + 1], scalar2=None, op0=ALU.is_equal)
                            dve("tensor_scalar", [iota_b.b, I2T.b, gT.b], [P2g[i2].b], out=P2g[i2].t[:, tt, :],
                                in0=iota_b.t[:], scalar1=I2T.t[:, t_:t_ + 1], scalar2=gT.t[:, t_:t_ + 1],
                                op0=ALU.is_equal, op1=ALU.mult)
                        for g4 in range(TB // 4):
                            pg = ps_G[gi % 2]
                            gi += 1
                            for j in range(4):
                                tt = g4 * 4 + j
                                mm(pg.t[:, j, :], P1[i2].t[:, tt, :], P2g[i2].t[:, tt, :], True, True,
                                   [P1[i2].b, P2g[i2].b], [pg.b])
                            act("copy", [pg.b], [gbufs[sbi]], out=G.t[:, tk + g4 * 4:tk + g4 * 4 + 4, :], in_=pg.t[:])
                    if dbg is not None and mt == 0:
                        for bb in gbufs:
                            S._wait("sp", bb.w)
                        dump("G", G.t[:, 0:8, :].rearrange("p a b -> p (a b)"), gbufs[0], [128, 1024], BF16)
                    S.barrier([sem_ld, sem_dbg] + sem_lp + sem_U)

                with ExitStack() as pe_:
                    hg = [sbt(pe_, f"hg{i}", [128, 256]) for i in range(2)]
                    ab = [sbt(pe_, f"ab{i}", [128, 256], BF16) for i in range(2)]
                    hgb = [[Buf(), Buf()] for _ in range(2)]
                    abb = [[Buf(), Buf()] for _ in range(2)]
                    ps_f = pst(pe_, "ps_f", [128, 2, 512])
                    fgen = fprime_steps(mt - 1, pe_, ps_f) if mt > 0 else None

                    def fstep(n=1):
                        nonlocal fgen
                        for _ in range(n):
                            if fgen is None:
                                return
                            try:
                                next(fgen)
                            except StopIteration:
                                fgen = None

                    def emit_H(c):
                        s = c % NSLOT
                        if c >= NSLOT:
                            emit_load(c)
                        ph = ps_H[c % 2]
                        for kc in range(8):
                            mm(ph.t[:], Us[s].t[:, kc, :], x1T.t[:, kc, :], kc == 0, kc == 7, [Us[s].b, x1T.b], [ph.b])

                    def emit_V(c):
                        s = c % NSLOT
                        ph = ps_H[c % 2]
                        h_ = hg[c % 2]
                        a_ = ab[c % 2]
                        for b in range(2):
                            hb_ = hgb[c % 2][b]
                            ab_ = abb[c % 2][b]
                            tsl = slice(b * 128, (b + 1) * 128)
                            act("activation", [ph.b], [hb_], out=h_.t[:, tsl], in_=ph.t[:, tsl], func=AF.Gelu)
                            dve("tensor_tensor", [hb_] + gbufs, [ab_], out=a_.t[:, tsl], in0=h_.t[:, tsl],
                                in1=G.t[:, tsl, c], op=ALU.mult)
                            for half in range(2):
                                mm(acc.t[:, b * 2 + half, :], a_.t[:, tsl],
                                   Vs[s].t[:, half * 512:(half + 1) * 512], c == 0, c == 127, [ab_, Vs[s].b],
                                   [bacc[b * 2 + half]])

                    fstep()
                    emit_H(0)
                    for c in range(128):
                        if c + 1 < 128:
                            emit_H(c + 1)
                        emit_V(c)
                        if c >= 2:
                            fstep()
                    while fgen is not None:
                        fstep()
                    for b in range(2):
                        dve("scalar_tensor_tensor", [x1_sb.b, bacc[2 * b], bacc[2 * b + 1]], [r2p[b].b],
                            out=r2p[b].t[:], in0=x1_sb.t[:, b, :], scalar=ALPHA,
                            in1=acc.t[:, 2 * b:2 * b + 2, :].rearrange("p a b -> p (a b)"), op0=ALU.mult, op1=ALU.add)
                    S.barrier([sem_ld, sem_dbg, sem_st] + sem_U + sem_lp)
                    for bb in gbufs:
                        bb.w = None
                        bb.rs = []

        def phase_F_last(mtp):
            with ExitStack() as pe_:
                ps_f = pst(pe_, "ps_f", [128, 2, 512])
                run_interleaved([fprime_steps(mtp, pe_, ps_f)])
                S.barrier([sem_ld, sem_dbg, sem_st] + sem_lp)

        for mt in range(nmt):
            S.new_epoch()
            if stop == "setup":
                break
            phase_M(mt)
            if stop == "M":
                break
            phase_R(mt)
            if stop == "R":
                break
            phase_GEF(mt)
        if stop is None:
            phase_F_last(nmt - 1)
        nc.sync.wait_ge(sem_st[0], sem_st[1])
        if sem_dbg[1] > 0:
            nc.sync.wait_ge(sem_dbg[0], sem_dbg[1])
        build.stats = (S.n_inst, S.n_wait)
    return nc, dbg_outs


def _chunked(w):
    k, n = w.shape
    return np.ascontiguousarray(w.reshape(k // 128, 128, n).transpose(1, 0, 2).reshape(128, (k // 128) * n))


def make_shared_inputs(w_in, sinks, conv_w, conv_b, conv_ln_g, conv_ln_b, w_out, ln1_g, ln1_b, peer_query,
                       peer_keys, peer_u, peer_v, ple_proj, ple_gate, ln2_g, ln2_b):
    f = np.float32
    w_in = np.asarray(w_in[0], f)
    qcols = np.concatenate([np.arange(h * 64, (h + 1) * 64) for h in PERM])
    wqkv = np.concatenate([w_in[:, qcols], w_in[:, 512:768]], axis=1)
    wc = w_in[:, 768:1792]
    sh = {}
    sh["wqkv"] = _chunked(wqkv)
    sh["wc"] = _chunked(wc)
    wo = np.asarray(w_out[0], f)
    sh["wout"] = _chunked(np.concatenate([wo[qcols], wo[512:]], axis=0))
    sh["wq"] = _chunked(np.asarray(peer_query[0], f))
    keys = np.asarray(peer_keys[0], f)
    sh["keysT"] = np.ascontiguousarray(keys.transpose(3, 0, 1, 2).reshape(128, 16 * 128))
    sh["pleg"] = _chunked(np.asarray(ple_gate[0], f))
    sh["plep"] = _chunked(np.asarray(ple_proj[0], f))
    u = np.asarray(peer_u[0], f).reshape(128, 128, 8, 128)
    sh["U"] = np.ascontiguousarray(u.transpose(1, 3, 2, 0).reshape(16384, 1024))
    v = np.asarray(peer_v[0], f).reshape(128, 128, 1024)
    sh["V"] = np.ascontiguousarray(v.transpose(1, 0, 2).reshape(16384, 1024))
    sh["sinks"] = np.ascontiguousarray(np.asarray(sinks, f).reshape(8)[PERM].reshape(1, 8))
    cw = np.asarray(conv_w[0], f)
    sh["convw"] = np.ascontiguousarray(cw.reshape(31, 4, 128).transpose(2, 1, 0).reshape(128, 4 * 31))
    for name, arr in (("convb", conv_b), ("clg", conv_ln_g), ("clb", conv_ln_b)):
        sh[name] = np.ascontiguousarray(np.asarray(arr[0], f).reshape(4, 128).T)
    for name, arr in (("ln1g", ln1_g), ("ln1b", ln1_b), ("ln2g", ln2_g), ("ln2b", ln2_b)):
        sh[name] = np.asarray(arr[0], f).reshape(1, 1024)
    sh["ident"] = np.eye(128, dtype=f)
    sh["iota"] = np.broadcast_to(np.arange(128, dtype=f)[None, :], (128, 128)).copy()
    qi = np.arange(128)[:, None]
    si = np.arange(256)[None, :]
    diff = qi + 128 - si
    valid = (diff >= 0) & (diff < 128)
    sh["maskg"] = np.where(valid, 0.0, NEG).astype(f)
    sh["maskf"] = np.where(valid & (si >= 128), 0.0, NEG).astype(f)
    invf = (500000.0 ** (-np.arange(8, dtype=np.float32) * np.float32(2.0 / 16))).astype(f)
    sh["invf"] = np.broadcast_to(invf[None, :], (128, 8)).copy()
    return sh


def make_core_inputs(x, p, positions, core, ntok=TPC):
    f = np.float32
    xs = np.asarray(x, f).reshape(-1, 1024)[core * TPC:core * TPC + ntok]
    ps_ = np.asarray(p[0], f).reshape(-1, 256)[core * TPC:core * TPC + ntok]
    pos = np.asarray(positions, np.int32).reshape(-1)[core * TPC:core * TPC + ntok]
    if ntok < TPC:
        xs = np.concatenate([xs, np.zeros((TPC - ntok, 1024), f)])
        ps_ = np.concatenate([ps_, np.zeros((TPC - ntok, 256), f)])
        pos = np.concatenate([pos, np.zeros(TPC - ntok, np.int32)])
    d = {}
    d["x"] = np.ascontiguousarray(xs)
    d["xT"] = np.ascontiguousarray(xs.T.reshape(8, 128, TPC).transpose(1, 0, 2).reshape(128, 8 * TPC))
    d["pT"] = np.ascontiguousarray(ps_.T.reshape(2, 128, TPC).transpose(1, 0, 2).reshape(128, 2 * TPC))
    d["pos"] = np.ascontiguousarray(pos.reshape(32, 128).T)
    return d


_NC_CACHE = {}


def kernel(x, p, positions, w_in, sinks, conv_w, conv_b, conv_ln_g, conv_ln_b, w_out, ln1_g, ln1_b,
           peer_query, peer_keys, peer_u, peer_v, ple_proj, ple_gate, ln2_g, ln2_b):
    sh = make_shared_inputs(w_in, sinks, conv_w, conv_b, conv_ln_g, conv_ln_b, w_out, ln1_g, ln1_b, peer_query,
                            peer_keys, peer_u, peer_v, ple_proj, ple_gate, ln2_g, ln2_b)
    in_maps = []
    for c in range(NCORES):
        d = dict(sh)
        d.update(make_core_inputs(x, p, positions, c))
        in_maps.append(d)
    if "nc" not in _NC_CACHE:
        _NC_CACHE["nc"] = build()[0]
    res = run_bass_kernel_spmd(_NC_CACHE["nc"], in_maps, core_ids=list(range(NCORES)))
    out = np.concatenate([np.asarray(r["out"], np.float32) for r in res.results], axis=0)
    return out.reshape(16, 2048, 1024)
```

```python
import math
from contextlib import ExitStack

import numpy as np
import concourse.bass as bass
import concourse.mybir as mybir
from concourse.bass_utils import run_bass_kernel_spmd

F32 = mybir.dt.float32
BF16 = mybir.dt.bfloat16
U32 = mybir.dt.uint32
I32 = mybir.dt.int32
AF = mybir.ActivationFunctionType
ALU = mybir.AluOpType
AX = mybir.AxisListType

NCORES = 8
TPC = 4096
MT = 256
NMT = TPC // MT
ALPHA = 2.0 ** 0.25
EPS = 1e-5
NEG = -1e30
PERM = [0, 4, 1, 5, 2, 6, 3, 7]
NSLOT = 4
TB = 8


class Buf:
    __slots__ = ("name", "w", "rs")

    def __init__(self, name=""):
        self.name = name
        self.w = None
        self.rs = []


class Sched:
    ENG = ("pe", "act", "dve", "pool", "sp")

    def __init__(self, nc, es, n_epochs):
        self.nc = nc
        self.eng = {"pe": nc.tensor, "act": nc.scalar, "dve": nc.vector,
                    "pool": nc.gpsimd, "sp": nc.sync}
        self.sems = {e: [es.enter_context(nc.semaphore(f"c_{e}_{i}")) for i in range(n_epochs)]
                     for e in self.ENG}
        self.epoch = 0
        self.cnt = {e: 0 for e in self.ENG}
        self.waited = {e: {} for e in self.ENG}
        self.n_inst = 0
        self.n_wait = 0

    def new_epoch(self):
        for e in self.ENG:
            self.cnt[e] = 0
        self.epoch += 1

    def _wait(self, eng, dep):
        if dep is None:
            return
        if dep[0] == "e":
            _, pe, ep, n = dep
            if pe == "pe" and eng == "pe":
                return
            key = ("e", pe, ep)
            sem = self.sems[pe][ep]
        else:
            _, sem, n = dep
            key = ("d", id(sem))
        if self.waited[eng].get(key, 0) >= n:
            return
        self.waited[eng][key] = n
        self.eng[eng].wait_ge(sem, n)
        self.n_wait += 1

    def _wait_all(self, eng, reads, writes, extra=()):
        best = {}
        deps = list(extra)
        for b in reads:
            deps.append(b.w)
        for b in writes:
            deps.append(b.w)
            deps.extend(b.rs)
        for d in deps:
            if d is None:
                continue
            key = (d[0], d[1], d[2]) if d[0] == "e" else (d[0], id(d[1]))
            if key not in best or best[key][-1] < d[-1]:
                best[key] = d
        for d in best.values():
            self._wait(eng, d)

    def op(self, eng, fn, reads=(), writes=()):
        self._wait_all(eng, reads, writes)
        inst = fn()
        self.cnt[eng] += 1
        n = self.cnt[eng]
        inst.then_inc(self.sems[eng][self.epoch], 1)
        me = ("e", eng, self.epoch, n)
        for b in reads:
            b.rs.append(me)
        for b in writes:
            b.w = me
            b.rs = []
        self.n_inst += 1
        return me

    def dma(self, q, st, out, in_, reads=(), writes=(), extra=()):
        self._wait_all(q, reads, writes, extra)
        inst = self.eng[q].dma_start(out=out, in_=in_)
        st[1] += 16
        inst.then_inc(st[0], 16)
        me = ("d", st[0], st[1])
        for b in reads:
            b.rs.append(me)
        for b in writes:
            b.w = me
            b.rs = []
        self.n_inst += 1
        return me

    def barrier(self, dma_states=()):
        for e in self.ENG:
            for p in self.ENG:
                if p != e and self.cnt[p] > 0:
                    self._wait(e, ("e", p, self.epoch, self.cnt[p]))
            for st in dma_states:
                if st[1] > 0:
                    self._wait(e, ("d", st[0], st[1]))


class T:
    def __init__(self, t, name):
        self.t = t
        self.b = Buf(name)


def build(nmt=NMT, dbg=None, stop=None):
    nc = bass.Bass("TRN2", target_bir_lowering=False)

    def din(name, shape, dt=F32):
        return nc.dram_tensor(name, shape, dt, kind="ExternalInput").ap()

    def dscr(name, shape, dt=BF16):
        return nc.dram_tensor(name, shape, dt, kind="Internal").ap()

    x_d = din("x", [TPC, 1024])
    xT_d = din("xT", [128, 8 * TPC])
    pT_d = din("pT", [128, 2 * TPC])
    pos_d = din("pos", [128, 32], I32)
    wqkv_d = din("wqkv", [128, 8 * 768])
    wc_d = din("wc", [128, 8 * 1024])
    wout_d = din("wout", [128, 8 * 1024])
    wq_d = din("wq", [128, 8 * 2048])
    keysT_d = din("keysT", [128, 16 * 128])
    pleg_d = din("pleg", [128, 8 * 1024])
    plep_d = din("plep", [128, 2 * 1024])
    U_d = din("U", [16384, 1024])
    V_d = din("V", [16384, 1024])
    sinks_d = din("sinks", [1, 8])
    convw_d = din("convw", [128, 4 * 31])
    convb_d = din("convb", [128, 4])
    clg_d = din("clg", [128, 4])
    clb_d = din("clb", [128, 4])
    ln1g_d = din("ln1g", [1, 1024])
    ln1b_d = din("ln1b", [1, 1024])
    ln2g_d = din("ln2g", [1, 1024])
    ln2b_d = din("ln2b", [1, 1024])
    ident_d = din("ident", [128, 128])
    iota_d = din("iota", [128, 128])
    maskg_d = din("maskg", [128, 256])
    maskf_d = din("maskf", [128, 256])
    invf_d = din("invf", [128, 8])
    out_d = nc.dram_tensor("out", [TPC, 1024], F32, kind="ExternalOutput").ap()

    xT_s = dscr("xT_s", [128, 8 * TPC])
    pT_s = dscr("pT_s", [128, 2 * TPC])
    wqkv_s = dscr("wqkv_s", [128, 8 * 768])
    wc_s = dscr("wc_s", [128, 8 * 1024])
    wout_s = dscr("wout_s", [128, 8 * 1024])
    wq_s = dscr("wq_s", [128, 8 * 2048])
    keysT_s = dscr("keysT_s", [128, 16 * 128])
    pleg_s = dscr("pleg_s", [128, 8 * 1024])
    plep_s = dscr("plep_s", [128, 2 * 1024])
    U_s = dscr("U_s", [16384, 1024])
    V_s = dscr("V_s", [16384, 1024])

    dbg_outs = {}

    es = ExitStack()
    with es:
        n_epochs = nmt + 1
        S = Sched(nc, es, n_epochs)

        def newsem(name):
            return [es.enter_context(nc.semaphore(name)), 0]

        uid = [0]

        def sbt(stack, name, shape, dt=F32):
            uid[0] += 1
            return T(stack.enter_context(nc.sbuf_tensor(f"s{uid[0]}_{name}", shape, dt)), name)

        def pst(stack, name, shape, dt=F32):
            uid[0] += 1
            return T(stack.enter_context(nc.psum_tensor(f"p{uid[0]}_{name}", shape, dt)), name)

        def dve(fn, reads, writes, **kw):
            return S.op("dve", lambda: getattr(nc.vector, fn)(**kw), reads, writes)

        def act(fn, reads, writes, **kw):
            return S.op("act", lambda: getattr(nc.scalar, fn)(**kw), reads, writes)

        def pool(fn, reads, writes, **kw):
            return S.op("pool", lambda: getattr(nc.gpsimd, fn)(**kw), reads, writes)

        def mm(out, lhsT, rhs, start, stop, reads, writes):
            return S.op("pe", lambda: nc.tensor.matmul(out, lhsT=lhsT, rhs=rhs, start=start, stop=stop),
                        reads, writes)

        def tr(out, in_, ident, reads, writes):
            return S.op("pe", lambda: nc.tensor.transpose(out, in_, ident), reads, writes)

        sem_ld = newsem("ld")
        sem_pA = newsem("pA")
        sem_pB = newsem("pB")
        sem_st = newsem("st")
        sem_dbg = newsem("dbg")
        sem_U = [newsem(f"U{i}") for i in range(NSLOT)]

        def dump(name, tile_ap, buf, shape, dt=F32):
            if dbg is None or name not in dbg:
                return
            d = nc.dram_tensor("dbg_" + name, list(shape), dt, kind="ExternalOutput").ap()
            dbg_outs[name] = d
            S.dma("sp", sem_dbg, d, tile_ap, reads=[buf])

        def cast_dma(dst, src, st):
            inst = nc.gpsimd.dma_start(out=dst, in_=src)
            st[1] += 16
            inst.then_inc(st[0], 16)

        def cast_rows(dst, src, st):
            rl = 2048 if dst.shape[1] % 2048 == 0 else 1536
            cast_dma(dst.rearrange("p (k t) -> (p k) t", t=rl), src.rearrange("p (k t) -> (p k) t", t=rl), st)

        sem_pA1 = newsem("pA1")
        for dst, src in ((wc_s, wc_d), (wqkv_s, wqkv_d), (wout_s, wout_d)):
            cast_rows(dst, src, sem_pA1)
        xTs3 = xT_s.rearrange("p (k t) -> p k t", k=8)
        xTd3 = xT_d.rearrange("p (k t) -> p k t", k=8)
        cast_dma(xTs3[:, :, 0:MT], xTd3[:, :, 0:MT], sem_pA1)
        depA1 = ("d", sem_pA1[0], sem_pA1[1])
        nc.gpsimd.wait_ge(sem_pA1[0], sem_pA1[1])
        for dst, src in ((wq_s, wq_d), (keysT_s, keysT_d)):
            cast_rows(dst, src, sem_pA1)
        depA2a = ("d", sem_pA1[0], sem_pA1[1])
        nc.gpsimd.wait_ge(sem_pA1[0], sem_pA1[1])
        NPB = 64
        RPB = 8192 // NPB
        for i in range(NPB):
            for tbl_s, tbl_d in ((U_s, U_d), (V_s, V_d)):
                vs = tbl_s.rearrange("(a b) d -> a (b d)", b=2)
                vd = tbl_d.rearrange("(a b) d -> a (b d)", b=2)
                if sem_pB[1] >= 32:
                    nc.gpsimd.wait_ge(sem_pB[0], sem_pB[1] - 16)
                cast_dma(vs[i * RPB:(i + 1) * RPB, :], vd[i * RPB:(i + 1) * RPB, :], sem_pB)
        depB = ("d", sem_pB[0], sem_pB[1])
        nc.gpsimd.wait_ge(sem_pB[0], sem_pB[1])
        for dst, src in ((pleg_s, pleg_d), (plep_s, plep_d), (pT_s, pT_d)):
            cast_rows(dst, src, sem_pA)
        cast_dma(xTs3[:, :, MT:TPC], xTd3[:, :, MT:TPC], sem_pA)
        depA = ("d", sem_pA[0], sem_pA[1])

        ident_f = sbt(es, "ident_f", [128, 128])
        ident_b = sbt(es, "ident_b", [128, 128], BF16)
        iota = sbt(es, "iota", [128, 128])
        maskg = sbt(es, "maskg", [128, 256])
        maskf = sbt(es, "maskf", [128, 256])
        invf = sbt(es, "invf", [128, 8])
        posi = sbt(es, "posi", [128, 32], I32)
        sinks = sbt(es, "sinks", [128, 8])
        convw = sbt(es, "convw", [128, 4, 31])
        convb = sbt(es, "convb", [128, 4])
        clg = sbt(es, "clg", [128, 4])
        clb = sbt(es, "clb", [128, 4])
        keysT = sbt(es, "keysT", [128, 16, 128], BF16)
        cos_all = sbt(es, "cos_all", [128, 32, 8])
        sin_all = sbt(es, "sin_all", [128, 32, 8])
        ones_m = sbt(es, "ones_m", [128, 128])
        kTa = sbt(es, "kTa", [128, 17, 128], BF16)
        kTb = sbt(es, "kTb", [128, 17, 128], BF16)
        Vseq = sbt(es, "Vseq", [128, 17, 128], BF16)
        hbuf = sbt(es, "hbuf", [128, 4, 286], BF16)
        diagw = sbt(es, "diagw", [128, 4, 31, 128], BF16)
        x1_sb = sbt(es, "x1_sb", [128, 2, 1024])
        x1T = sbt(es, "x1T", [128, 8, 256], BF16)
        I1T = sbt(es, "I1T", [128, 256])
        I2T = sbt(es, "I2T", [128, 256])
        iota_b = sbt(es, "iota_b", [128, 128], BF16)
        gT = sbt(es, "gT", [128, 256])
        Us = [sbt(es, f"Us{i}", [128, 8, 128], BF16) for i in range(NSLOT)]
        Vs = [sbt(es, f"Vs{i}", [128, 1024], BF16) for i in range(NSLOT)]

        ld_group = []
        sem_lp = [newsem(f"lp{i}") for i in range(5)]

        def ld(tile, src, extra=(), k=None, dst=None):
            if k is not None:
                S.dma("sp", sem_lp[k], tile.t[:] if dst is None else dst, src, writes=[tile.b], extra=extra)
                return
            S.dma("sp", sem_ld, tile.t[:], src, writes=[tile.b], extra=extra)
            ld_group.append(tile)

        def ld_commit():
            for tl in ld_group:
                tl.b.w = ("d", sem_ld[0], sem_ld[1])
            ld_group.clear()

        ld(ident_f, ident_d[:, :])
        ld(iota, iota_d[:, :])
        ld(maskg, maskg_d[:, :])
        ld(maskf, maskf_d[:, :])
        ld(invf, invf_d[:, :])
        ld(posi, pos_d[:, :])
        ld(sinks, sinks_d[0:1, :].to_broadcast([128, 8]))
        ld(convw, convw_d.rearrange("p (c k) -> p c k", c=4))
        ld(convb, convb_d[:, :])
        ld(clg, clg_d[:, :])
        ld(clb, clb_d[:, :])
        ld_commit()

        with ExitStack() as ps_:
            dve("tensor_copy", [ident_f.b], [ident_b.b], out=ident_b.t[:], in_=ident_f.t[:])
            dve("tensor_copy", [iota.b], [iota_b.b], out=iota_b.t[:], in_=iota.t[:])
            dve("memset", [], [ones_m.b], ap=ones_m.t[:], constant=1.0 / 512)
            dve("memset", [], [kTa.b], ap=kTa.t[:], constant=0.0)
            dve("memset", [], [kTb.b], ap=kTb.t[:], constant=0.0)
            dve("memset", [], [Vseq.b], ap=Vseq.t[:], constant=0.0)
            dve("memset", [], [hbuf.b], ap=hbuf.t[:], constant=0.0)
            for cc in range(4):
                for k in range(31):
                    dve("tensor_scalar", [ident_f.b, convw.b], [diagw.b], out=diagw.t[:, cc, k, :], in0=ident_f.t[:],
                        scalar1=convw.t[:, cc, k:k + 1], scalar2=None, op0=ALU.mult)
            posf = sbt(ps_, "posf", [128, 32])
            ang = sbt(ps_, "ang", [128, 32, 8])
            tq = sbt(ps_, "tq", [128, 32, 8])
            ki = sbt(ps_, "ki", [128, 32, 8], I32)
            dve("tensor_copy", [posi.b], [posf.b], out=posf.t[:], in_=posi.t[:])
            dve("tensor_tensor", [posf.b, invf.b], [ang.b], out=ang.t[:],
                in0=posf.t[:].unsqueeze(2).to_broadcast([128, 32, 8]),
                in1=invf.t[:].unsqueeze(1).to_broadcast([128, 32, 8]), op=ALU.mult)
            TWO_PI = 2.0 * math.pi

            def range_reduce_sin(dst, shift):
                u = sbt(ps_, f"u{shift:.2f}", [128, 32, 8])
                dve("tensor_scalar", [ang.b], [u.b], out=u.t[:], in0=ang.t[:], scalar1=float(shift), scalar2=None,
                    op0=ALU.add)
                dve("tensor_scalar", [u.b], [tq.b], out=tq.t[:], in0=u.t[:], scalar1=1.0 / TWO_PI, scalar2=None,
                    op0=ALU.mult)
                dve("tensor_copy", [tq.b], [ki.b], out=ki.t[:], in_=tq.t[:])
                dve("tensor_copy", [ki.b], [tq.b], out=tq.t[:], in_=ki.t[:])
                dve("scalar_tensor_tensor", [tq.b, u.b], [u.b], out=u.t[:], in0=tq.t[:], scalar=-TWO_PI,
                    in1=u.t[:], op0=ALU.mult, op1=ALU.add)
                dve("tensor_single_scalar", [u.b], [tq.b], out=tq.t[:], in_=u.t[:], scalar=math.pi, op=ALU.is_gt)
                dve("scalar_tensor_tensor", [tq.b, u.b], [u.b], out=u.t[:], in0=tq.t[:], scalar=-TWO_PI,
                    in1=u.t[:], op0=ALU.mult, op1=ALU.add)
                dve("tensor_single_scalar", [u.b], [tq.b], out=tq.t[:], in_=u.t[:], scalar=-math.pi, op=ALU.is_lt)
                dve("scalar_tensor_tensor", [tq.b, u.b], [u.b], out=u.t[:], in0=tq.t[:], scalar=TWO_PI,
                    in1=u.t[:], op0=ALU.mult, op1=ALU.add)
                dve("tensor_scalar", [u.b], [u.b], out=u.t[:], in0=u.t[:], scalar1=-3.1415925, scalar2=3.1415925,
                    op0=ALU.max, op1=ALU.min)
                act("activation", [u.b], [dst.b], out=dst.t[:], in_=u.t[:], func=AF.Sin)

            range_reduce_sin(sin_all, 0.0)
            range_reduce_sin(cos_all, math.pi / 2)
            S.barrier([sem_ld])

        def ln_steps(stack, r_ap, r_b, g, b_, out_ap, out_buf, tag):
            stats = sbt(stack, "st_" + tag, [128, 2, 6])
            mv = sbt(stack, "mv_" + tag, [128, 2])
            rstd = sbt(stack, "rs_" + tag, [128, 1])
            for i in range(2):
                dve("bn_stats", [r_b], [stats.b], out=stats.t[:, i, :], in_=r_ap[:, i * 512:(i + 1) * 512])
                yield
            dve("bn_aggr", [stats.b], [mv.b], out=mv.t[:], in_=stats.t[:])
            yield
            dve("tensor_scalar", [mv.b], [rstd.b], out=rstd.t[:], in0=mv.t[:, 1:2], scalar1=EPS, scalar2=None,
                op0=ALU.add)
            yield
            act("activation", [rstd.b], [rstd.b], out=rstd.t[:], in_=rstd.t[:], func=AF.Ln)
            yield
            act("activation", [rstd.b], [rstd.b], out=rstd.t[:], in_=rstd.t[:], func=AF.Exp, scale=-0.5)
            yield
            dve("tensor_scalar", [r_b, mv.b, rstd.b], [r_b], out=r_ap, in0=r_ap, scalar1=mv.t[:, 0:1],
                scalar2=rstd.t[:, 0:1], op0=ALU.subtract, op1=ALU.mult)
            yield
            dve("tensor_tensor", [r_b, g.b], [r_b], out=r_ap, in0=r_ap, in1=g.t[:], op=ALU.mult)
            yield
            dve("tensor_tensor", [r_b, b_.b], [out_buf], out=out_ap, in0=r_ap, in1=b_.t[:], op=ALU.add)
            yield

        def run_interleaved(gens):
            gens = list(gens)
            while gens:
                for g_ in list(gens):
                    try:
                        next(g_)
                    except StopIteration:
                        gens.remove(g_)

        def layer_norm(stack, r, g, b_, out_ap, out_buf, tag):
            run_interleaved([ln_steps(stack, r.t[:], r.b, g, b_, out_ap, out_buf, tag)])

        def phase_M(mt):
            ms = mt % 8
            t0 = mt * MT
            with ExitStack() as pe_:
                wqkv = sbt(pe_, "wqkv", [128, 8, 768], BF16)
                wc = sbt(pe_, "wc", [128, 8, 1024], BF16)
                wout = sbt(pe_, "wout", [128, 8, 1024], BF16)
                xT = sbt(pe_, "xT", [128, 8, 256], BF16)
                x_sb = sbt(pe_, "x_sb", [128, 2, 1024])
                ld(xT, xT_s.rearrange("p (k t) -> p k t", k=8)[:, :, t0:t0 + MT],
                   extra=[depA1] if mt == 0 else [depA1, depA], k=0)
                ld(wc, wc_s.rearrange("p (k c) -> p k c", k=8), k=1)
                ld(wqkv, wqkv_s.rearrange("p (k c) -> p k c", k=8), k=2)
                ld(wout, wout_s.rearrange("p (k c) -> p k c", k=8), k=3)
                ld(x_sb, x_d[t0:t0 + MT, :].rearrange("(b p) d -> p b d", p=128), k=4)
                ln1g = sbt(pe_, "ln1g", [128, 1024])
                ln1b = sbt(pe_, "ln1b", [128, 1024])
                ld(ln1g, ln1g_d[0:1, :].to_broadcast([128, 1024]), k=4)
                ld(ln1b, ln1b_d[0:1, :].to_broadcast([128, 1024]), k=4)
                x_sb.b.w = ln1g.b.w = ln1b.b.w

                ps_qkv = pst(pe_, "ps_qkv", [128, 2, 512])
                ps_tr = pst(pe_, "ps_tr", [128, 8, 128], BF16)
                ps_sc = pst(pe_, "ps_sc", [128, 4, 256])
                ps_o = pst(pe_, "ps_o", [128, 8, 64])
                ps_c = [pst(pe_, "ps_c0", [128, 2, 256])] * 2
                ps_stat = pst(pe_, "ps_stat", [128, 2, 256])

                qkv_sb = sbt(pe_, "qkv_sb", [128, 768])
                rt = [sbt(pe_, f"rt{i}", [128, 10, 8]) for i in range(4)]
                qkv_bf = sbt(pe_, "qkv_bf", [128, 640], BF16)
                qT = sbt(pe_, "qT", [128, 4, 128], BF16)
                sc4 = sbt(pe_, "sc4", [128, 4, 256])
                P4 = [sbt(pe_, f"P4_{i}", [128, 4, 256], BF16) for i in range(2)]
                PT4 = [sbt(pe_, f"PT4_{i}", [128, 8, 128], BF16) for i in range(2)]
                m4 = [sbt(pe_, f"m4_{i}", [128, 7, 4]) for i in range(2)]
                cat_bf = sbt(pe_, "cat_bf", [128, 512], BF16)
                catT = sbt(pe_, "catT", [128, 8, 256], BF16)
                sig = [sbt(pe_, "sig0", [128, 256])] * 2
                cacc = [sbt(pe_, f"cacc{i}", [128, 256]) for i in range(4)]
                ysq = sbt(pe_, "ysq", [128, 256])
                mean_sb = sbt(pe_, "mean_sb", [128, 256])
                rstd_c = sbt(pe_, "rstd_c", [128, 256])
                r_sb = sbt(pe_, "r_sb", [128, 1024])
                x1_bf = sbt(pe_, "x1_bf", [128, 1024], BF16)
                hb = [Buf(f"hb{i}") for i in range(4)]

                if ms == 0:
                    for cc in range(4):
                        dve("memset", [hbuf.b], [hb[cc]], ap=hbuf.t[:, cc, 0:30], constant=0.0)
                else:
                    for cc in range(4):
                        dve("tensor_copy", [hbuf.b], [hb[cc]], out=hbuf.t[:, cc, 0:30], in_=hbuf.t[:, cc, 256:286])

                for cc in range(4):
                    pc = ps_c[cc % 2]
                    for j, col0 in enumerate((cc * 128, 512 + cc * 128)):
                        for kc in range(8):
                            mm(pc.t[:, j, :], wc.t[:, kc, col0:col0 + 128], xT.t[:, kc, :], kc == 0, kc == 7,
                               [wc.b, xT.b], [pc.b])
                    sg = sig[cc % 2]
                    act("activation", [pc.b], [sg.b], out=sg.t[:], in_=pc.t[:, 1, :], func=AF.Sigmoid)
                    dve("tensor_tensor", [pc.b, sg.b, hb[cc]], [hb[cc]], out=hbuf.t[:, cc, 30:286], in0=pc.t[:, 0, :],
                        in1=sg.t[:], op=ALU.mult)
                ps_cv = ps_qkv.t[:].rearrange("p a (b t) -> p (a b) t", b=2)
                for cc in range(4):
                    for k in range(31):
                        mm(ps_cv[:, cc, :], diagw.t[:, cc, k, :], hbuf.t[:, cc, k:k + 256], k == 0, k == 30,
                           [diagw.b, hb[cc]], [ps_qkv.b])
                for cc in range(4):
                    act("activation", [ps_qkv.b, convb.b], [cacc[cc].b], out=cacc[cc].t[:], in_=ps_cv[:, cc, :],
                        func=AF.Identity, bias=convb.t[:, cc:cc + 1], scale=1.0)

                for cc in range(4):
                    mm(ps_stat.t[:, 0, :], ones_m.t[:], cacc[cc].t[:], cc == 0, cc == 3, [ones_m.b, cacc[cc].b],
                       [ps_stat.b])
                for cc in range(4):
                    act("activation", [cacc[cc].b], [ysq.b], out=ysq.t[:], in_=cacc[cc].t[:], func=AF.Square)
                    mm(ps_stat.t[:, 1, :], ones_m.t[:], ysq.t[:], cc == 0, cc == 3, [ones_m.b, ysq.b], [ps_stat.b])
                act("copy", [ps_stat.b], [mean_sb.b], out=mean_sb.t[:], in_=ps_stat.t[:, 0, :])
                dve("tensor_tensor", [mean_sb.b], [ysq.b], out=ysq.t[:], in0=mean_sb.t[:], in1=mean_sb.t[:], op=ALU.mult)
                dve("scalar_tensor_tensor", [ps_stat.b, ysq.b], [rstd_c.b], out=rstd_c.t[:], in0=ps_stat.t[:, 1, :],
                    scalar=EPS, in1=ysq.t[:], op0=ALU.add, op1=ALU.subtract)
                act("activation", [rstd_c.b], [rstd_c.b], out=rstd_c.t[:], in_=rstd_c.t[:], func=AF.Ln)
                act("activation", [rstd_c.b], [rstd_c.b], out=rstd_c.t[:], in_=rstd_c.t[:], func=AF.Exp, scale=-0.5)
                for cc in range(4):
                    dve("tensor_tensor", [cacc[cc].b, mean_sb.b], [cacc[cc].b], out=cacc[cc].t[:], in0=cacc[cc].t[:],
                        in1=mean_sb.t[:], op=ALU.subtract)
                    dve("tensor_tensor", [cacc[cc].b, rstd_c.b], [cacc[cc].b], out=cacc[cc].t[:], in0=cacc[cc].t[:],
                        in1=rstd_c.t[:], op=ALU.mult)
                    act("activation", [cacc[cc].b, clg.b, clb.b], [catT.b], out=catT.t[:, 4 + cc, :], in_=cacc[cc].t[:],
                        func=AF.Silu, scale=clg.t[:, cc:cc + 1], bias=clb.t[:, cc:cc + 1])
                for b in range(2):
                    gblk = 2 * ms + b
                    sl = gblk + 1
                    blk = mt * 2 + b
                    for kc in range(8):
                        mm(ps_qkv.t[:, 0, :], xT.t[:, kc, b * 128:(b + 1) * 128], wqkv.t[:, kc, 0:512], kc == 0,
                           kc == 7, [xT.b, wqkv.b], [ps_qkv.b])
                    for kc in range(8):
                        mm(ps_qkv.t[:, 1, 0:256], xT.t[:, kc, b * 128:(b + 1) * 128], wqkv.t[:, kc, 512:768],
                           kc == 0, kc == 7, [xT.b, wqkv.b], [ps_qkv.b])
                    act("copy", [ps_qkv.b], [qkv_sb.b], out=qkv_sb.t[:, 0:512], in_=ps_qkv.t[:, 0, :])
                    act("copy", [ps_qkv.b], [qkv_sb.b], out=qkv_sb.t[:, 512:768], in_=ps_qkv.t[:, 1, 0:256])
                    qk = qkv_sb.t[:, 0:640].rearrange("p (h c) -> p h c", c=64)
                    t1 = qk[:, :, 0:8]
                    t2 = qk[:, :, 8:16]
                    cs = cos_all.t[:, blk, :].unsqueeze(1).to_broadcast([128, 10, 8])
                    sn = sin_all.t[:, blk, :].unsqueeze(1).to_broadcast([128, 10, 8])
                    dve("tensor_tensor", [qkv_sb.b, cos_all.b], [rt[0].b], out=rt[0].t[:], in0=t1, in1=cs, op=ALU.mult)
                    dve("tensor_tensor", [qkv_sb.b, sin_all.b], [rt[1].b], out=rt[1].t[:], in0=t2, in1=sn, op=ALU.mult)
                    dve("tensor_tensor", [qkv_sb.b, cos_all.b], [rt[2].b], out=rt[2].t[:], in0=t2, in1=cs, op=ALU.mult)
                    dve("tensor_tensor", [qkv_sb.b, sin_all.b], [rt[3].b], out=rt[3].t[:], in0=t1, in1=sn, op=ALU.mult)
                    dve("tensor_tensor", [rt[0].b, rt[1].b], [qkv_sb.b], out=t1, in0=rt[0].t[:], in1=rt[1].t[:],
                        op=ALU.subtract)
                    dve("tensor_tensor", [rt[2].b, rt[3].b], [qkv_sb.b], out=t2, in0=rt[2].t[:], in1=rt[3].t[:],
                        op=ALU.add)
                    act("copy", [qkv_sb.b], [qkv_bf.b], out=qkv_bf.t[:], in_=qkv_sb.t[:, 0:640])
                    act("copy", [qkv_sb.b], [Vseq.b], out=Vseq.t[:, sl, :], in_=qkv_sb.t[:, 640:768])
                    for j in range(5):
                        tr(ps_tr.t[:, j, :], qkv_bf.t[:, j * 128:(j + 1) * 128], ident_b.t[:],
                           [qkv_bf.b, ident_b.b], [ps_tr.b])
                    dve("tensor_copy", [ps_tr.b], [qT.b], out=qT.t[:], in_=ps_tr.t[:, 0:4, :])
                    dve("tensor_copy", [ps_tr.b], [kTa.b], out=kTa.t[0:64, sl, :], in_=ps_tr.t[0:64, 4, :])
                    dve("tensor_copy", [ps_tr.b], [kTb.b], out=kTb.t[64:128, sl, :], in_=ps_tr.t[64:128, 4, :])
                    if dbg is not None and mt == 0 and b == 0:
                        dump("qkv", qkv_sb.t[:], qkv_sb.b, [128, 768])
                    mask = maskf if gblk == 0 else maskg
                    for g in range(2):
                        P4g, PT4g, m4g = P4[g], PT4[g], m4[g]
                        for k4 in range(4):
                            hi = 4 * g + k4
                            jq, hf = hi // 2, hi % 2
                            kT = kTa if hf == 0 else kTb
                            mm(ps_sc.t[:, k4, :], qT.t[:, jq, :],
                               kT.t[:, sl - 1:sl + 1, :].rearrange("p a b -> p (a b)"), True, True, [qT.b, kT.b],
                               [ps_sc.b])
                        dve("scalar_tensor_tensor", [ps_sc.b, mask.b], [sc4.b], out=sc4.t[:], in0=ps_sc.t[:],
                            scalar=0.125, in1=mask.t[:].unsqueeze(1).to_broadcast([128, 4, 256]), op0=ALU.mult,
                            op1=ALU.add)
                        dve("tensor_reduce", [sc4.b], [m4g.b], out=m4g.t[:, 0, :], in_=sc4.t[:], axis=AX.X, op=ALU.max)
                        dve("tensor_tensor", [m4g.b, sinks.b], [m4g.b], out=m4g.t[:, 1, :], in0=m4g.t[:, 0, :],
                            in1=sinks.t[:, 4 * g:4 * g + 4], op=ALU.max)
                        dve("tensor_scalar", [m4g.b], [m4g.b], out=m4g.t[:, 2, :], in0=m4g.t[:, 1, :], scalar1=-1.0,
                            scalar2=None, op0=ALU.mult)
                        dve("tensor_tensor", [sc4.b, m4g.b], [sc4.b], out=sc4.t[:], in0=sc4.t[:],
                            in1=m4g.t[:, 2, :].unsqueeze(2).to_broadcast([128, 4, 256]), op=ALU.add)
                        act("activation", [sc4.b], [P4g.b], out=P4g.t[:], in_=sc4.t[:], func=AF.Exp)
                        dve("tensor_reduce", [P4g.b], [m4g.b], out=m4g.t[:, 3, :], in_=P4g.t[:], axis=AX.X, op=ALU.add)
                        dve("tensor_tensor", [m4g.b, sinks.b], [m4g.b], out=m4g.t[:, 4, :], in0=m4g.t[:, 2, :],
                            in1=sinks.t[:, 4 * g:4 * g + 4], op=ALU.add)
                        act("activation", [m4g.b], [m4g.b], out=m4g.t[:, 4, :], in_=m4g.t[:, 4, :], func=AF.Exp)
                        dve("tensor_tensor", [m4g.b], [m4g.b], out=m4g.t[:, 5, :], in0=m4g.t[:, 3, :], in1=m4g.t[:, 4, :],
                            op=ALU.add)
                        dve("reciprocal", [m4g.b], [m4g.b], out=m4g.t[:, 6, :], in_=m4g.t[:, 5, :])
                        for k4 in range(4):
                            for j2 in range(2):
                                tr(ps_tr.t[:, k4 * 2 + j2, :], P4g.t[:, k4, j2 * 128:(j2 + 1) * 128], ident_b.t[:],
                                   [P4g.b, ident_b.b], [ps_tr.b])
                        act("copy", [ps_tr.b], [PT4g.b], out=PT4g.t[:], in_=ps_tr.t[:])
                        for k4 in range(4):
                            hf = (4 * g + k4) % 2
                            for j2 in range(2):
                                mm(ps_o.t[:, 4 * g + k4, :], PT4g.t[:, k4 * 2 + j2, :],
                                   Vseq.t[:, sl - 1 + j2, hf * 64:(hf + 1) * 64], j2 == 0, j2 == 1, [PT4g.b, Vseq.b],
                                   [ps_o.b])
                        dve("tensor_tensor", [ps_o.b, m4g.b], [cat_bf.b],
                            out=cat_bf.t[:, g * 256:(g + 1) * 256].rearrange("p (k d) -> p k d", k=4),
                            in0=ps_o.t[:, 4 * g:4 * g + 4, :], in1=m4g.t[:, 6, :].unsqueeze(2).to_broadcast([128, 4, 64]),
                            op=ALU.mult)
                    for j in range(4):
                        tr(ps_tr.t[:, j, :], cat_bf.t[:, j * 128:(j + 1) * 128], ident_b.t[:],
                           [cat_bf.b, ident_b.b], [ps_tr.b])
                    act("copy", [ps_tr.b], [catT.b], out=catT.t[:, 0:4, b * 128:(b + 1) * 128], in_=ps_tr.t[:, 0:4, :])

                if dbg is not None and mt == 0:
                    dump("catT", catT.t[:].rearrange("p a b -> p (a b)"), catT.b, [128, 2048], BF16)

                ps_mix = [(ps_qkv.t[:].rearrange("p a b -> p (a b)"), ps_qkv.b),
                          (ps_sc.t[:].rearrange("p a b -> p (a b)"), ps_sc.b)]
                r_bufs = [(r_sb.t[:], r_sb.b), (sc4.t[:].rearrange("p a b -> p (a b)"), sc4.b)]
                for b in range(2):
                    pm_ap, pm_b = ps_mix[b]
                    for half in range(2):
                        for fc in range(8):
                            mm(pm_ap[:, half * 512:(half + 1) * 512], catT.t[:, fc, b * 128:(b + 1) * 128],
                               wout.t[:, fc, half * 512:(half + 1) * 512], fc == 0, fc == 7, [catT.b, wout.b], [pm_b])
                for b in range(2):
                    pm_ap, pm_b = ps_mix[b]
                    r_ap, r_b = r_bufs[b]
                    dve("scalar_tensor_tensor", [x_sb.b, pm_b], [r_b], out=r_ap, in0=x_sb.t[:, b, :],
                        scalar=ALPHA, in1=pm_ap, op0=ALU.mult, op1=ALU.add)
                x1b = [Buf("x1b0"), Buf("x1b1")]
                run_interleaved([ln_steps(pe_, r_bufs[b][0], r_bufs[b][1], ln1g, ln1b, x1_sb.t[:, b, :], x1b[b],
                                          f"m{b}") for b in range(2)])
                for b in range(2):
                    act("copy", [x1b[b]], [x1_bf.b], out=x1_bf.t[:], in_=x1_sb.t[:, b, :])
                    for j in range(8):
                        tr(ps_tr.t[:, j, :], x1_bf.t[:, j * 128:(j + 1) * 128], ident_b.t[:], [x1_bf.b, ident_b.b],
                           [ps_tr.b])
                    dve("tensor_copy", [ps_tr.b], [x1T.b], out=x1T.t[:, :, b * 128:(b + 1) * 128], in_=ps_tr.t[:])
                dve("memset", [x1b[0], x1b[1]], [x1_sb.b], ap=m4[0].t[:, 0, 0:1], constant=0.0)
                if dbg is not None and mt == 0:
                    dump("x1", x1_sb.t[:].rearrange("p a b -> p (a b)"), x1_sb.b, [128, 2048])
                hbuf.b.w = None
                hbuf.b.rs = []
                S.barrier([sem_ld, sem_dbg] + sem_lp)

        def phase_R(mt):
            with ExitStack() as pe_:
                wq = sbt(pe_, "wq", [128, 8, 2048], BF16)
                wq_parts = [T(wq.t, f"wq{i}") for i in range(4)]
                wq_v = wq_s.rearrange("p (k c) -> p k c", k=8)
                if mt == 0:
                    ld(keysT, keysT_s.rearrange("p (h n) -> p h n", h=16), extra=[depA2a], k=4)
                for i in range(4):
                    ld(wq_parts[i], wq_v[:, :, i * 512:(i + 1) * 512], extra=[depA2a], k=i,
                       dst=wq.t[:, :, i * 512:(i + 1) * 512])
                qpT = sbt(pe_, "qpT", [128, 16, 256], BF16)
                ps_qp = [pst(pe_, f"ps_qp{i}", [128, 256]) for i in range(2)]
                ps_s = [pst(pe_, f"ps_s{i}", [128, 4, 128]) for i in range(4)]
                ps_t3 = pst(pe_, "ps_t3", [128, 3, 128])
                sc = sbt(pe_, "sc", [128, 16, 128])
                scw = sbt(pe_, "scw", [128, 16, 128])
                top = sbt(pe_, "top", [128, 16, 16])
                idxu = sbt(pe_, "idxu", [128, 16, 16], U32)
                idxf = sbt(pe_, "idxf", [128, 16, 16])
                cand = sbt(pe_, "cand", [128, 8, 256])
                candw = sbt(pe_, "candw", [128, 8, 256])
                best = sbt(pe_, "best", [128, 8, 16])
                cidx = sbt(pe_, "cidx", [128, 8, 16], U32)
                kk = sbt(pe_, "kk", [128, 2, 128], U32)
                kkf = sbt(pe_, "kkf", [128, 2, 128])
                oh = sbt(pe_, "oh", [128, 8, 16, 16])
                IG = sbt(pe_, "IG", [128, 3, 128])
                ee = sbt(pe_, "ee", [128, 8, 16])
                zz = sbt(pe_, "zz", [128, 2, 8])

                for hp in range(16):
                    pq = ps_qp[hp % 2]
                    for kc in range(8):
                        mm(pq.t[:], wq.t[:, kc, hp * 128:(hp + 1) * 128], x1T.t[:, kc, :], kc == 0, kc == 7,
                           [wq_parts[hp // 4].b, x1T.b], [pq.b])
                    if hp % 2 == 0:
                        act("copy", [pq.b], [qpT.b], out=qpT.t[:, hp, :], in_=pq.t[:])
                    else:
                        dve("tensor_copy", [pq.b], [qpT.b], out=qpT.t[:, hp, :], in_=pq.t[:])
                for b in range(2):
                    for hp in range(16):
                        pss = ps_s[hp // 4]
                        mm(pss.t[:, hp % 4, :], qpT.t[:, hp, b * 128:(b + 1) * 128], keysT.t[:, hp, :], True, True,
                           [qpT.b, keysT.b], [pss.b])
                    for q4 in range(4):
                        act("copy", [ps_s[q4].b], [sc.b], out=sc.t[:, q4 * 4:(q4 + 1) * 4, :], in_=ps_s[q4].t[:])
                    if dbg is not None and mt == 0 and b == 0:
                        dump("sc", sc.t[:].rearrange("p a b -> p (a b)"), sc.b, [128, 2048])
                    tb = [Buf() for _ in range(16)]
                    ib = [Buf() for _ in range(16)]
                    wb = [Buf() for _ in range(16)]
                    for hp in range(16):
                        dve("max", [sc.b], [tb[hp]], out=top.t[:, hp, 0:8], in_=sc.t[:, hp, :])
                    for hp in range(16):
                        dve("max_index", [sc.b, tb[hp]], [ib[hp]], out=idxu.t[:, hp, 0:8], in_max=top.t[:, hp, 0:8],
                            in_values=sc.t[:, hp, :])
                    for hp in range(16):
                        dve("match_replace", [sc.b, tb[hp]], [wb[hp]], out=scw.t[:, hp, :],
                            in_to_replace=top.t[:, hp, 0:8], in_values=sc.t[:, hp, :], imm_value=NEG)
                    for hp in range(16):
                        dve("max", [wb[hp]], [tb[hp]], out=top.t[:, hp, 8:16], in_=scw.t[:, hp, :])
                    for hp in range(16):
                        dve("max_index", [wb[hp], tb[hp]], [ib[hp]], out=idxu.t[:, hp, 8:16], in_max=top.t[:, hp, 8:16],
                            in_values=scw.t[:, hp, :])
                    dve("tensor_copy", ib + [idxu.b], [idxf.b, idxu.b], out=idxf.t[:], in_=idxu.t[:])
                    top.b.w = None
                    top.b.rs = []
                    top4 = top.t[:].rearrange("p (h q) k -> p h q k", q=2)
                    idx4 = idxf.t[:].rearrange("p (h q) k -> p h q k", q=2)
                    cand4 = cand.t[:].rearrange("p h (a b) -> p h a b", a=16)
                    dve("tensor_tensor", tb + wb + [top.b, scw.b], [cand.b, top.b, scw.b], out=cand4,
                        in0=top4[:, :, 0, :].unsqueeze(3).to_broadcast([128, 8, 16, 16]),
                        in1=top4[:, :, 1, :].unsqueeze(2).to_broadcast([128, 8, 16, 16]), op=ALU.add)
                    bb_ = [Buf() for _ in range(8)]
                    cb_ = [Buf() for _ in range(8)]
                    cw_ = [Buf() for _ in range(8)]
                    for h in range(8):
                        dve("max", [cand.b], [bb_[h]], out=best.t[:, h, 0:8], in_=cand.t[:, h, :])
                    for h in range(8):
                        dve("max_index", [cand.b, bb_[h]], [cb_[h]], out=cidx.t[:, h, 0:8], in_max=best.t[:, h, 0:8],
                            in_values=cand.t[:, h, :])
                    for h in range(8):
                        dve("match_replace", [cand.b, bb_[h]], [cw_[h]], out=candw.t[:, h, :],
                            in_to_replace=best.t[:, h, 0:8], in_values=cand.t[:, h, :], imm_value=NEG)
                    for h in range(8):
                        dve("max", [cw_[h]], [bb_[h]], out=best.t[:, h, 8:16], in_=candw.t[:, h, :])
                    for h in range(8):
                        dve("max_index", [cw_[h], bb_[h]], [cb_[h]], out=cidx.t[:, h, 8:16],
                            in_max=best.t[:, h, 8:16], in_values=candw.t[:, h, :])
                    best.b.w = None
                    best.b.rs = []
                    cflat = cidx.t[:].rearrange("p h k -> p (h k)")
                    dve("tensor_single_scalar", cb_ + cw_ + [cidx.b, candw.b], [kk.b, cidx.b, candw.b], out=kk.t[:, 0, :],
                        in_=cflat, scalar=4, op=ALU.logical_shift_right)
                    dve("tensor_single_scalar", [cidx.b], [kk.b], out=kk.t[:, 1, :], in_=cflat, scalar=15,
                        op=ALU.bitwise_and)
                    dve("tensor_copy", [kk.b], [kkf.b], out=kkf.t[:], in_=kk.t[:])
                    for q in range(2):
                        kq = kkf.t[:, q, :].rearrange("p (h k) -> p h k", h=8)
                        dve("tensor_tensor", [kkf.b, iota.b], [oh.b], out=oh.t[:],
                            in0=kq.unsqueeze(3).to_broadcast([128, 8, 16, 16]),
                            in1=iota.t[:, 0:16].unsqueeze(1).unsqueeze(1).to_broadcast([128, 8, 16, 16]),
                            op=ALU.is_equal)
                        dve("tensor_tensor", [oh.b, idxf.b], [oh.b], out=oh.t[:], in0=oh.t[:],
                            in1=idx4[:, :, q, :].unsqueeze(2).to_broadcast([128, 8, 16, 16]), op=ALU.mult)
                        dve("tensor_reduce", [oh.b], [IG.b], out=IG.t[:, q, :].rearrange("p (h k) -> p h k", h=8),
                            in_=oh.t[:], axis=AX.X, op=ALU.add)
                    dve("tensor_tensor", bb_ + [best.b], [ee.b, best.b], out=ee.t[:], in0=best.t[:],
                        in1=best.t[:, :, 0:1].to_broadcast([128, 8, 16]), op=ALU.subtract)
                    act("activation", [ee.b], [ee.b], out=ee.t[:], in_=ee.t[:], func=AF.Exp)
                    dve("tensor_reduce", [ee.b], [zz.b], out=zz.t[:, 0, :], in_=ee.t[:], axis=AX.X, op=ALU.add)
                    dve("reciprocal", [zz.b], [zz.b], out=zz.t[:, 1, :], in_=zz.t[:, 0, :])
                    dve("tensor_tensor", [ee.b, zz.b], [IG.b], out=IG.t[:, 2, :].rearrange("p (h k) -> p h k", h=8),
                        in0=ee.t[:], in1=zz.t[:, 1, :].unsqueeze(2).to_broadcast([128, 8, 16]), op=ALU.mult)
                    if dbg is not None and mt == 0 and b == 0:
                        dump("IG", IG.t[:].rearrange("p a b -> p (a b)"), IG.b, [128, 384])
                    for q in range(3):
                        tr(ps_t3.t[:, q, :], IG.t[:, q, :], ident_f.t[:], [IG.b, ident_f.b], [ps_t3.b])
                    act("copy", [ps_t3.b], [I1T.b], out=I1T.t[:, b * 128:(b + 1) * 128], in_=ps_t3.t[:, 0, :])
                    act("copy", [ps_t3.b], [I2T.b], out=I2T.t[:, b * 128:(b + 1) * 128], in_=ps_t3.t[:, 1, :])
                    act("copy", [ps_t3.b], [gT.b], out=gT.t[:, b * 128:(b + 1) * 128], in_=ps_t3.t[:, 2, :])
                S.barrier([sem_ld, sem_dbg] + sem_lp)

        r2p = [sbt(es, f"r2p{i}", [128, 1024]) for i in range(2)]

        def fprime_steps(mtp, st_, ps_f):
            t0p = mtp * MT
            pleg = sbt(st_, "pleg", [128, 8, 1024], BF16)
            plep = sbt(st_, "plep", [128, 2, 1024], BF16)
            pT = sbt(st_, "pT", [128, 2, 256], BF16)
            ln2g = sbt(st_, "ln2g", [128, 1024])
            ln2b = sbt(st_, "ln2b", [128, 1024])
            ld(pleg, pleg_s.rearrange("p (k c) -> p k c", k=8), extra=[depA], k=0)
            ld(plep, plep_s.rearrange("p (k c) -> p k c", k=2), k=1)
            ld(pT, pT_s.rearrange("p (k t) -> p k t", k=2)[:, :, t0p:t0p + MT], k=2)
            ld(ln2g, ln2g_d[0:1, :].to_broadcast([128, 1024]), k=3)
            ld(ln2b, ln2b_d[0:1, :].to_broadcast([128, 1024]), k=3)
            ln2g.b.w = ln2b.b.w
            yield
            r2T = sbt(st_, "r2T", [128, 8, 128], BF16)
            sg = sbt(st_, "sgf", [128, 1024])
            o_sb = sbt(st_, "o_sb", [128, 1024])
            psf_flat = ps_f.t[:].rearrange("p a b -> p (a b)")
            psf_tr = ps_f.t[:].rearrange("p a (c d) -> p (a c) d", c=4)
            for b in range(2):
                for j in range(8):
                    tr(psf_tr[:, j, :], r2p[b].t[:, j * 128:(j + 1) * 128], ident_f.t[:], [r2p[b].b, ident_f.b],
                       [ps_f.b])
                    yield
                act("copy", [ps_f.b], [r2T.b], out=r2T.t[:], in_=psf_tr)
                yield
                for half in range(2):
                    for kc in range(8):
                        mm(ps_f.t[:, half, :], r2T.t[:, kc, :], pleg.t[:, kc, half * 512:(half + 1) * 512], kc == 0,
                           kc == 7, [r2T.b, pleg.b], [ps_f.b])
                        yield
                for hh in range(2):
                    act("activation", [ps_f.b], [sg.b], out=sg.t[:, hh * 512:(hh + 1) * 512], in_=ps_f.t[:, hh, :],
                        func=AF.Tanh, scale=0.5)
                    yield
                for half in range(2):
                    for kc in range(2):
                        mm(ps_f.t[:, half, :], pT.t[:, kc, b * 128:(b + 1) * 128],
                           plep.t[:, kc, half * 512:(half + 1) * 512], kc == 0, kc == 1, [pT.b, plep.b], [ps_f.b])
                        yield
                dve("scalar_tensor_tensor", [sg.b, ps_f.b], [sg.b], out=sg.t[:], in0=sg.t[:], scalar=1.0, in1=psf_flat,
                    op0=ALU.add, op1=ALU.mult)
                yield
                dve("scalar_tensor_tensor", [sg.b, r2p[b].b], [sg.b], out=sg.t[:], in0=sg.t[:], scalar=0.5,
                    in1=r2p[b].t[:], op0=ALU.mult, op1=ALU.add)
                yield
                yield from ln_steps(st_, sg.t[:], sg.b, ln2g, ln2b, o_sb.t[:], o_sb.b, f"f{b}")
                S.dma("sp", sem_st, out_d[t0p + b * 128:t0p + (b + 1) * 128, :], o_sb.t[:], reads=[o_sb.b])
                yield

        def phase_GEF(mt):
            with ExitStack() as po_:
                acc = pst(po_, "acc", [128, 4, 512])
                bacc = [Buf(f"acc{i}") for i in range(4)]
                G = sbt(po_, "G", [128, 256, 128], BF16)
                ps_H = [pst(po_, f"ps_H{i}", [128, 256]) for i in range(2)]
                gbufs = [Buf(f"G{i}") for i in range(MT // TB)]

                def emit_load(c):
                    s_ = c % NSLOT
                    S.dma("sp", sem_U[s_], Us[s_].t[:],
                          U_s[c * 128:(c + 1) * 128, :].rearrange("p (k e) -> p k e", k=8), writes=[Us[s_].b],
                          extra=[depB])
                    S.dma("sp", sem_U[s_], Vs[s_].t[:], V_s[c * 128:(c + 1) * 128, :], writes=[Vs[s_].b])
                    Us[s_].b.w = Vs[s_].b.w

                for c in range(NSLOT):
                    emit_load(c)
                with ExitStack() as pe_:
                    P1 = [sbt(pe_, f"P1_{i}", [128, TB, 128], BF16) for i in range(2)]
                    P2g = [sbt(pe_, f"P2g_{i}", [128, TB, 128], BF16) for i in range(2)]
                    ps_G = [pst(pe_, f"ps_G{i}", [128, 4, 128]) for i in range(2)]
                    gi = 0
                    for sbi in range(MT // TB):
                        tk = sbi * TB
                        i2 = sbi % 2
                        for tt in range(TB):
                            t_ = tk + tt
                            dve("tensor_scalar", [iota_b.b, I1T.b], [P1[i2].b], out=P1[i2].t[:, tt, :], in0=iota_b.t[:],
                                scalar1=I1T.t[:, t_:t_ + 1], scalar2=None, op0=ALU.is_equal)
                            dve("tensor_scalar", [iota_b.b, I2T.b, gT.b], [P2g[i2].b], out=P2g[i2].t[:, tt, :],
                                in0=iota_b.t[:], scalar1=I2T.t[:, t_:t_ + 1], scalar2=gT.t[:, t_:t_ + 1],
                                op0=ALU.is_equal, op1=ALU.mult)
                        for g4 in range(TB // 4):
                            pg = ps_G[gi % 2]
                            gi += 1
                            for j in range(4):
                                tt = g4 * 4 + j
                                mm(pg.t[:, j, :], P1[i2].t[:, tt, :], P2g[i2].t[:, tt, :], True, True,
                                   [P1[i2].b, P2g[i2].b], [pg.b])
                            act("copy", [pg.b], [gbufs[sbi]], out=G.t[:, tk + g4 * 4:tk + g4 * 4 + 4, :], in_=pg.t[:])
                    if dbg is not None and mt == 0:
                        for bb in gbufs:
                            S._wait("sp", bb.w)
                        dump("G", G.t[:, 0:8, :].rearrange("p a b -> p (a b)"), gbufs[0], [128, 1024], BF16)
                    S.barrier([sem_ld, sem_dbg] + sem_lp + sem_U)

                with ExitStack() as pe_:
                    hg = [sbt(pe_, f"hg{i}", [128, 256]) for i in range(2)]
                    ab = [sbt(pe_, f"ab{i}", [128, 256], BF16) for i in range(2)]
                    hgb = [[Buf(), Buf()] for _ in range(2)]
                    abb = [[Buf(), Buf()] for _ in range(2)]
                    ps_f = pst(pe_, "ps_f", [128, 2, 512])
                    fgen = fprime_steps(mt - 1, pe_, ps_f) if mt > 0 else None

                    def fstep(n=1):
                        nonlocal fgen
                        for _ in range(n):
                            if fgen is None:
                                return
                            try:
                                next(fgen)
                            except StopIteration:
                                fgen = None

                    def emit_H(c):
                        s = c % NSLOT
                        if c >= NSLOT:
                            emit_load(c)
                        ph = ps_H[c % 2]
                        for kc in range(8):
                            mm(ph.t[:], Us[s].t[:, kc, :], x1T.t[:, kc, :], kc == 0, kc == 7, [Us[s].b, x1T.b], [ph.b])

                    def emit_V(c):
                        s = c % NSLOT
                        ph = ps_H[c % 2]
                        h_ = hg[c % 2]
                        a_ = ab[c % 2]
                        for b in range(2):
                            hb_ = hgb[c % 2][b]
                            ab_ = abb[c % 2][b]
                            tsl = slice(b * 128, (b + 1) * 128)
                            act("activation", [ph.b], [hb_], out=h_.t[:, tsl], in_=ph.t[:, tsl], func=AF.Gelu)
                            dve("tensor_tensor", [hb_] + gbufs, [ab_], out=a_.t[:, tsl], in0=h_.t[:, tsl],
                                in1=G.t[:, tsl, c], op=ALU.mult)
                            for half in range(2):
                                mm(acc.t[:, b * 2 + half, :], a_.t[:, tsl],
                                   Vs[s].t[:, half * 512:(half + 1) * 512], c == 0, c == 127, [ab_, Vs[s].b],
                                   [bacc[b * 2 + half]])

                    fstep()
                    emit_H(0)
                    for c in range(128):
                        if c + 1 < 128:
                            emit_H(c + 1)
                        emit_V(c)
                        if c >= 2:
                            fstep()
                    while fgen is not None:
                        fstep()
                    for b in range(2):
                        dve("scalar_tensor_tensor", [x1_sb.b, bacc[2 * b], bacc[2 * b + 1]], [r2p[b].b],
                            out=r2p[b].t[:], in0=x1_sb.t[:, b, :], scalar=ALPHA,
                            in1=acc.t[:, 2 * b:2 * b + 2, :].rearrange("p a b -> p (a b)"), op0=ALU.mult, op1=ALU.add)
                    S.barrier([sem_ld, sem_dbg, sem_st] + sem_U + sem_lp)
                    for bb in gbufs:
                        bb.w = None
                        bb.rs = []

        def phase_F_last(mtp):
            with ExitStack() as pe_:
                ps_f = pst(pe_, "ps_f", [128, 2, 512])
                run_interleaved([fprime_steps(mtp, pe_, ps_f)])
                S.barrier([sem_ld, sem_dbg, sem_st] + sem_lp)

        for mt in range(nmt):
            S.new_epoch()
            if stop == "setup":
                break
            phase_M(mt)
            if stop == "M":
                break
            phase_R(mt)
            if stop == "R":
                break
            phase_GEF(mt)
        if stop is None:
            phase_F_last(nmt - 1)
        nc.sync.wait_ge(sem_st[0], sem_st[1])
        if sem_dbg[1] > 0:
            nc.sync.wait_ge(sem_dbg[0], sem_dbg[1])
        build.stats = (S.n_inst, S.n_wait)
    return nc, dbg_outs


def _chunked(w):
    k, n = w.shape
    return np.ascontiguousarray(w.reshape(k // 128, 128, n).transpose(1, 0, 2).reshape(128, (k // 128) * n))


def make_shared_inputs(w_in, sinks, conv_w, conv_b, conv_ln_g, conv_ln_b, w_out, ln1_g, ln1_b, peer_query,
                       peer_keys, peer_u, peer_v, ple_proj, ple_gate, ln2_g, ln2_b):
    f = np.float32
    w_in = np.asarray(w_in[0], f)
    qcols = np.concatenate([np.arange(h * 64, (h + 1) * 64) for h in PERM])
    wqkv = np.concatenate([w_in[:, qcols], w_in[:, 512:768]], axis=1)
    wc = w_in[:, 768:1792]
    sh = {}
    sh["wqkv"] = _chunked(wqkv)
    sh["wc"] = _chunked(wc)
    wo = np.asarray(w_out[0], f)
    sh["wout"] = _chunked(np.concatenate([wo[qcols], wo[512:]], axis=0))
    sh["wq"] = _chunked(np.asarray(peer_query[0], f))
    keys = np.asarray(peer_keys[0], f)
    sh["keysT"] = np.ascontiguousarray(keys.transpose(3, 0, 1, 2).reshape(128, 16 * 128))
    sh["pleg"] = _chunked(np.asarray(ple_gate[0], f))
    sh["plep"] = _chunked(np.asarray(ple_proj[0], f))
    u = np.asarray(peer_u[0], f).reshape(128, 128, 8, 128)
    sh["U"] = np.ascontiguousarray(u.transpose(1, 3, 2, 0).reshape(16384, 1024))
    v = np.asarray(peer_v[0], f).reshape(128, 128, 1024)
    sh["V"] = np.ascontiguousarray(v.transpose(1, 0, 2).reshape(16384, 1024))
    sh["sinks"] = np.ascontiguousarray(np.asarray(sinks, f).reshape(8)[PERM].reshape(1, 8))
    cw = np.asarray(conv_w[0], f)
    sh["convw"] = np.ascontiguousarray(cw.reshape(31, 4, 128).transpose(2, 1, 0).reshape(128, 4 * 31))
    for name, arr in (("convb", conv_b), ("clg", conv_ln_g), ("clb", conv_ln_b)):
        sh[name] = np.ascontiguousarray(np.asarray(arr[0], f).reshape(4, 128).T)
    for name, arr in (("ln1g", ln1_g), ("ln1b", ln1_b), ("ln2g", ln2_g), ("ln2b", ln2_b)):
        sh[name] = np.asarray(arr[0], f).reshape(1, 1024)
    sh["ident"] = np.eye(128, dtype=f)
    sh["iota"] = np.broadcast_to(np.arange(128, dtype=f)[None, :], (128, 128)).copy()
    qi = np.arange(128)[:, None]
    si = np.arange(256)[None, :]
    diff = qi + 128 - si
    valid = (diff >= 0) & (diff < 128)
    sh["maskg"] = np.where(valid, 0.0, NEG).astype(f)
    sh["maskf"] = np.where(valid & (si >= 128), 0.0, NEG).astype(f)
    invf = (500000.0 ** (-np.arange(8, dtype=np.float32) * np.float32(2.0 / 16))).astype(f)
    sh["invf"] = np.broadcast_to(invf[None, :], (128, 8)).copy()
    return sh


def make_core_inputs(x, p, positions, core, ntok=TPC):
    f = np.float32
    xs = np.asarray(x, f).reshape(-1, 1024)[core * TPC:core * TPC + ntok]
    ps_ = np.asarray(p[0], f).reshape(-1, 256)[core * TPC:core * TPC + ntok]
    pos = np.asarray(positions, np.int32).reshape(-1)[core * TPC:core * TPC + ntok]
    if ntok < TPC:
        xs = np.concatenate([xs, np.zeros((TPC - ntok, 1024), f)])
        ps_ = np.concatenate([ps_, np.zeros((TPC - ntok, 256), f)])
        pos = np.concatenate([pos, np.zeros(TPC - ntok, np.int32)])
    d = {}
    d["x"] = np.ascontiguousarray(xs)
    d["xT"] = np.ascontiguousarray(xs.T.reshape(8, 128, TPC).transpose(1, 0, 2).reshape(128, 8 * TPC))
    d["pT"] = np.ascontiguousarray(ps_.T.reshape(2, 128, TPC).transpose(1, 0, 2).reshape(128, 2 * TPC))
    d["pos"] = np.ascontiguousarray(pos.reshape(32, 128).T)
    return d


_NC_CACHE = {}


def kernel(x, p, positions, w_in, sinks, conv_w, conv_b, conv_ln_g, conv_ln_b, w_out, ln1_g, ln1_b,
           peer_query, peer_keys, peer_u, peer_v, ple_proj, ple_gate, ln2_g, ln2_b):
    sh = make_shared_inputs(w_in, sinks, conv_w, conv_b, conv_ln_g, conv_ln_b, w_out, ln1_g, ln1_b, peer_query,
                            peer_keys, peer_u, peer_v, ple_proj, ple_gate, ln2_g, ln2_b)
    in_maps = []
    for c in range(NCORES):
        d = dict(sh)
        d.update(make_core_inputs(x, p, positions, c))
        in_maps.append(d)
    if "nc" not in _NC_CACHE:
        _NC_CACHE["nc"] = build()[0]
    res = run_bass_kernel_spmd(_NC_CACHE["nc"], in_maps, core_ids=list(range(NCORES)))
    out = np.concatenate([np.asarray(r["out"], np.float32) for r in res.results], axis=0)
    return out.reshape(16, 2048, 1024)
```
